# Optimizing a Trainium2 kernel written in Bass

```python
import math
import jax, jax.numpy as jnp
from jax import lax
import numpy as np

D_MODEL = 2048
BATCH = 1
SEQ = 16384
DEPTH = 2

FOX_HEADS = 8
FOX_HEAD_DIM = 128
FOX_WIDTH = FOX_HEADS * FOX_HEAD_DIM
Q_BLOCK = 128
SSM_WIDTH = D_MODEL // 2
SSM_GROUP = 16
SSM_GROUPS = SSM_WIDTH // SSM_GROUP
SSM_STATE = 64
SWA_HEAD_DIM = 64
SWA_Q_HEADS = D_MODEL // SWA_HEAD_DIM
SWA_Q_PER_KV = 8
SWA_KV_HEADS = SWA_Q_HEADS // SWA_Q_PER_KV
WINDOW = 128
SWA_BLOCK = WINDOW
ROT_DIM = SWA_HEAD_DIM // 4
ROPE_THETA = 500000.0
D_FF = ((8 * D_MODEL + 3 * 256 - 1) // (3 * 256)) * 256
DEEPNORM_ALPHA = (2 * DEPTH) ** 0.25
DEEPNORM_BETA = (8 * DEPTH) ** -0.25
LN_EPS = 1e-5
N_EVEN = (DEPTH + 1) // 2
N_ODD = DEPTH // 2
EVEN_IN = 3 * FOX_WIDTH + FOX_HEADS + SSM_WIDTH
ODD_IN = (SWA_Q_HEADS + 2 * SWA_KV_HEADS) * SWA_HEAD_DIM

kernel_name = 'hybrid_fox_s5_swa_deepnorm_adaln'


def layer_norm(x, g, b):
    xf = x.astype(jnp.float32)
    mu = jnp.mean(xf, axis=-1, keepdims=True)
    var = jnp.mean(jnp.square(xf - mu), axis=-1, keepdims=True)
    y = (xf - mu) * lax.rsqrt(var + LN_EPS) * g.astype(jnp.float32) + b.astype(jnp.float32)
    return y.astype(x.dtype)


def partial_rotary(x, positions):
    half = ROT_DIM // 2
    inv_freq = jnp.power(jnp.float32(ROPE_THETA), -jnp.arange(half, dtype=jnp.float32) * (2.0 / ROT_DIM))
    ang = positions.astype(jnp.float32)[:, :, None] * inv_freq
    cos = jnp.cos(ang)[:, :, None, :]
    sin = jnp.sin(ang)[:, :, None, :]
    xf = x.astype(jnp.float32)
    x1 = xf[..., :half]
    x2 = xf[..., half:ROT_DIM]
    out = jnp.concatenate([x1 * cos - x2 * sin, x2 * cos + x1 * sin, xf[..., ROT_DIM:]], axis=-1)
    return out.astype(x.dtype)


def forgetting_attention(q, k, v, log_f):
    B, L, H, d = q.shape
    nb = L // Q_BLOCK
    scale = 1.0 / math.sqrt(d)
    Fk = lax.cumsum(log_f, axis=1).transpose(0, 2, 1)
    Fq_blocks = Fk.reshape(B, H, nb, Q_BLOCK).transpose(2, 0, 1, 3)
    q_blocks = q.reshape(B, nb, Q_BLOCK, H, d).transpose(1, 0, 2, 3, 4)
    kpos = jnp.arange(L)

    def one_block(args):
        qb, Fq, n = args
        s = jnp.einsum('bqhd,bshd->bhqs', qb, k).astype(jnp.float32) * scale
        s = s + Fq[:, :, :, None] - Fk[:, :, None, :]
        qpos = n * Q_BLOCK + jnp.arange(Q_BLOCK)
        causal = kpos[None, :] <= qpos[:, None]
        s = jnp.where(causal[None, None], s, -jnp.inf)
        p = jax.nn.softmax(s, axis=-1)
        return jnp.einsum('bhqs,bshd->bqhd', p.astype(v.dtype), v)

    out = lax.map(one_block, (q_blocks, Fq_blocks, jnp.arange(nb)))
    return out.transpose(1, 0, 2, 3, 4).reshape(B, L, H * d)


def _complex_affine_combine(e1, e2):
    a1r, a1i, b1r, b1i = e1
    a2r, a2i, b2r, b2i = e2
    ar = a2r * a1r - a2i * a1i
    ai = a2r * a1i + a2i * a1r
    br = a2r * b1r - a2i * b1i + b2r
    bi = a2r * b1i + a2i * b1r + b2i
    return (ar, ai, br, bi)


def s5_ssm(u, lam_re, lam_im, log_dt, b_re, b_im, c_re, c_im, d_skip):
    B_, L, _ = u.shape
    f32 = jnp.float32
    uf = u.astype(f32).reshape(B_, L, SSM_GROUPS, SSM_GROUP)
    lam_re = lam_re.astype(f32)
    lam_im = lam_im.astype(f32)
    dt = jnp.exp(log_dt.astype(f32))[:, None]
    mag = jnp.exp(lam_re * dt)
    lb_re = mag * jnp.cos(lam_im * dt)
    lb_im = mag * jnp.sin(lam_im * dt)
    den = lam_re * lam_re + lam_im * lam_im
    nr = lb_re - 1.0
    q_re = (nr * lam_re + lb_im * lam_im) / den
    q_im = (lb_im * lam_re - nr * lam_im) / den
    br = b_re.astype(f32)
    bi = b_im.astype(f32)
    bb_re = q_re[..., None] * br - q_im[..., None] * bi
    bb_im = q_re[..., None] * bi + q_im[..., None] * br
    bu_re = jnp.einsum('blgh,gph->blgp', uf, bb_re)
    bu_im = jnp.einsum('blgh,gph->blgp', uf, bb_im)
    a_re = jnp.broadcast_to(lb_re, bu_re.shape)
    a_im = jnp.broadcast_to(lb_im, bu_im.shape)
    _, _, x_re, x_im = lax.associative_scan(_complex_affine_combine, (a_re, a_im, bu_re, bu_im), axis=1)
    y = (jnp.einsum('ghp,blgp->blgh', c_re.astype(f32), x_re)
         - jnp.einsum('ghp,blgp->blgh', c_im.astype(f32), x_im)
         + d_skip.astype(f32) * uf)
    return y.reshape(B_, L, SSM_WIDTH)


def even_mixer(h, w_in, b_forget, lam_re, lam_im, log_dt, b_re, b_im, c_re, c_im, d_skip, w_glu, b_glu, w_out):
    B, L, _ = h.shape
    proj = h @ w_in
    W = FOX_WIDTH
    q = proj[..., 0:W].reshape(B, L, FOX_HEADS, FOX_HEAD_DIM)
    k = proj[..., W:2 * W].reshape(B, L, FOX_HEADS, FOX_HEAD_DIM)
    v = proj[..., 2 * W:3 * W].reshape(B, L, FOX_HEADS, FOX_HEAD_DIM)
    f_logit = proj[..., 3 * W:3 * W + FOX_HEADS].astype(jnp.float32) + b_forget.astype(jnp.float32)
    u = proj[..., 3 * W + FOX_HEADS:]
    a_out = forgetting_attention(q, k, v, jax.nn.log_sigmoid(f_logit))
    s = jax.nn.gelu(s5_ssm(u, lam_re, lam_im, log_dt, b_re, b_im, c_re, c_im, d_skip))
    s = s * jax.nn.sigmoid(s @ w_glu.astype(jnp.float32) + b_glu.astype(jnp.float32))
    return jnp.concatenate([a_out, s.astype(h.dtype)], axis=-1) @ w_out


def sliding_window_sink_attention(q, k, v, sinks):
    B, L, Hq, dh = q.shape
    nb = L // SWA_BLOCK
    scale = 1.0 / math.sqrt(dh)
    qb = q.reshape(B, nb, SWA_BLOCK, SWA_KV_HEADS, SWA_Q_PER_KV, dh)
    kb = k.reshape(B, nb, SWA_BLOCK, SWA_KV_HEADS, dh)
    vb = v.reshape(B, nb, SWA_BLOCK, SWA_KV_HEADS, dh)
    zk = jnp.zeros_like(kb[:, :1])
    kk = jnp.concatenate([jnp.concatenate([zk, kb[:, :-1]], axis=1), kb], axis=2)
    vv = jnp.concatenate([jnp.concatenate([zk, vb[:, :-1]], axis=1), vb], axis=2)
    s = jnp.einsum('bnqkgd,bnskd->bnkgqs', qb, kk).astype(jnp.float32) * scale
    qi = jnp.arange(SWA_BLOCK)[:, None]
    kj = jnp.arange(2 * SWA_BLOCK)[None, :]
    rel = qi + SWA_BLOCK - kj
    band = (rel >= 0) & (rel < WINDOW)
    valid = (jnp.arange(nb)[:, None, None] * SWA_BLOCK + kj[None] - SWA_BLOCK) >= 0
    mask = band[None] & valid
    s = jnp.where(mask[None, :, None, None], s, -jnp.inf)
    sink = sinks.astype(jnp.float32).reshape(SWA_KV_HEADS, SWA_Q_PER_KV)[None, None, :, :, None, None]
    m = jnp.maximum(jnp.max(s, axis=-1, keepdims=True), sink)
    p = jnp.exp(s - m)
    p = p / (jnp.sum(p, axis=-1, keepdims=True) + jnp.exp(sink - m))
    out = jnp.einsum('bnkgqs,bnskd->bnqkgd', p.astype(v.dtype), vv)
    return out.reshape(B, L, Hq * dh)


def odd_mixer(h, positions, w_in, sinks, w_out):
    B, L, _ = h.shape
    proj = h @ w_in
    qw = SWA_Q_HEADS * SWA_HEAD_DIM
    kw = SWA_KV_HEADS * SWA_HEAD_DIM
    q = proj[..., :qw].reshape(B, L, SWA_Q_HEADS, SWA_HEAD_DIM)
    k = proj[..., qw:qw + kw].reshape(B, L, SWA_KV_HEADS, SWA_HEAD_DIM)
    v = proj[..., qw + kw:].reshape(B, L, SWA_KV_HEADS, SWA_HEAD_DIM)
    q = partial_rotary(q, positions)
    k = partial_rotary(k, positions)
    return sliding_window_sink_attention(q, k, v, sinks) @ w_out


def swiglu(h, w_gate, w_up, w_down):
    return (jax.nn.silu(h @ w_gate) * (h @ w_up)) @ w_down


def setup_inputs(seed: int = 0) -> dict:
    key = jax.random.key(seed)
    ks = jax.random.split(key, 32)
    f32 = jnp.float32

    def nrm(k, shape, scale):
        return jax.random.normal(k, shape, f32) * scale

    G, P, H = SSM_GROUPS, SSM_STATE, SSM_GROUP
    n_idx = jnp.arange(P, dtype=f32)
    mix_w = FOX_WIDTH + SSM_WIDTH
    return {
        'x': nrm(ks[0], (BATCH, SEQ, D_MODEL), 1.0),
        'c': nrm(ks[1], (BATCH, D_MODEL), 1.0),
        'positions': jnp.broadcast_to(jnp.arange(SEQ, dtype=jnp.int32)[None, :], (BATCH, SEQ)),
        'w_in_ab': nrm(ks[2], (N_EVEN, D_MODEL, EVEN_IN), D_MODEL ** -0.5),
        'b_forget': 3.0 + nrm(ks[3], (N_EVEN, FOX_HEADS), 0.1),
        'ssm_lambda_re': -0.5 + nrm(ks[4], (N_EVEN, G, P), 0.01),
        'ssm_lambda_im': math.pi * n_idx + nrm(ks[5], (N_EVEN, G, P), 0.01),
        'ssm_log_dt': jax.random.uniform(ks[6], (N_EVEN, G), f32, math.log(1e-3), math.log(1e-1)),
        'ssm_b_re': nrm(ks[7], (N_EVEN, G, P, H), (2 * H) ** -0.5),
        'ssm_b_im': nrm(ks[8], (N_EVEN, G, P, H), (2 * H) ** -0.5),
        'ssm_c_re': nrm(ks[9], (N_EVEN, G, H, P), P ** -0.5),
        'ssm_c_im': nrm(ks[10], (N_EVEN, G, H, P), P ** -0.5),
        'ssm_d': nrm(ks[11], (N_EVEN, G, H), 1.0),
        'w_glu': nrm(ks[12], (N_EVEN, SSM_WIDTH, SSM_WIDTH), SSM_WIDTH ** -0.5),
        'b_glu': nrm(ks[13], (N_EVEN, SSM_WIDTH), 0.01),
        'w_out_ab': nrm(ks[14], (N_EVEN, mix_w, D_MODEL), DEEPNORM_BETA * mix_w ** -0.5),
        'w_in_c': nrm(ks[15], (N_ODD, D_MODEL, ODD_IN), D_MODEL ** -0.5),
        'attn_sinks': nrm(ks[16], (N_ODD, SWA_Q_HEADS), 0.5),
        'w_out_c': nrm(ks[17], (N_ODD, SWA_Q_HEADS * SWA_HEAD_DIM, D_MODEL), DEEPNORM_BETA * (SWA_Q_HEADS * SWA_HEAD_DIM) ** -0.5),
        'w_ada': nrm(ks[18], (DEPTH, D_MODEL, 6 * D_MODEL), 0.5 * D_MODEL ** -0.5),
        'b_ada': nrm(ks[19], (DEPTH, 6 * D_MODEL), 0.01),
        'ln_mix_g': 1.0 + nrm(ks[20], (DEPTH, D_MODEL), 0.02),
        'ln_mix_b': nrm(ks[21], (DEPTH, D_MODEL), 0.01),
        'ln_ffn_g': 1.0 + nrm(ks[22], (DEPTH, D_MODEL), 0.02),
        'ln_ffn_b': nrm(ks[23], (DEPTH, D_MODEL), 0.01),
        'w_ffn_gate': nrm(ks[24], (DEPTH, D_MODEL, D_FF), D_MODEL ** -0.5),
        'w_ffn_up': nrm(ks[25], (DEPTH, D_MODEL, D_FF), D_MODEL ** -0.5),
        'w_ffn_down': nrm(ks[26], (DEPTH, D_FF, D_MODEL), DEEPNORM_BETA * D_FF ** -0.5),
    }


def reference(x, c, positions, w_in_ab, b_forget, ssm_lambda_re, ssm_lambda_im, ssm_log_dt,
              ssm_b_re, ssm_b_im, ssm_c_re, ssm_c_im, ssm_d, w_glu, b_glu, w_out_ab,
              w_in_c, attn_sinks, w_out_c, w_ada, b_ada, ln_mix_g, ln_mix_b, ln_ffn_g, ln_ffn_b,
              w_ffn_gate, w_ffn_up, w_ffn_down):
    for layer in range(DEPTH):
        mod = jax.nn.silu(c) @ w_ada[layer] + b_ada[layer]
        sh1, sc1, g1, sh2, sc2, g2 = jnp.split(mod, 6, axis=-1)
        h = x * (1.0 + sc1[:, None, :]) + sh1[:, None, :]
        i = layer // 2
        if layer % 2 == 0:
            y = even_mixer(h, w_in_ab[i], b_forget[i], ssm_lambda_re[i], ssm_lambda_im[i], ssm_log_dt[i],
                           ssm_b_re[i], ssm_b_im[i], ssm_c_re[i], ssm_c_im[i], ssm_d[i],
                           w_glu[i], b_glu[i], w_out_ab[i])
        else:
            y = odd_mixer(h, positions, w_in_c[i], attn_sinks[i], w_out_c[i])
        x = layer_norm(DEEPNORM_ALPHA * x + g1[:, None, :] * y, ln_mix_g[layer], ln_mix_b[layer])
        h = x * (1.0 + sc2[:, None, :]) + sh2[:, None, :]
        y = swiglu(h, w_ffn_gate[layer], w_ffn_up[layer], w_ffn_down[layer])
        x = layer_norm(DEEPNORM_ALPHA * x + g2[:, None, :] * y, ln_ffn_g[layer], ln_ffn_b[layer])
    return x
```

```python
import math
import contextlib
import numpy as np
import ml_dtypes
import concourse.bass as bass
import concourse.mybir as mybir
from concourse.bass_utils import run_bass_kernel_spmd

F32 = mybir.dt.float32
BF16 = mybir.dt.bfloat16
I32 = mybir.dt.int32
AF = mybir.ActivationFunctionType
ALU = mybir.AluOpType

D = 2048
KC = 16
DFF = 5632
JC = 44
NCORE = 8
ALPHA = math.sqrt(2.0)
LN_EPS = 1e-5
HALO = 128
TWO_PI = 2.0 * math.pi
CW1 = 6.28125
CW2 = TWO_PI - CW1
GELU_C = math.sqrt(2.0 / math.pi)


class Buf:
    __slots__ = ("name", "w", "r")

    def __init__(self, name=""):
        self.name = name
        self.w = None
        self.r = []


class Op:
    __slots__ = ("eng", "fn", "deps", "signal", "cnt", "semkey", "is_dma", "inc")

    def __init__(self, eng, fn, is_dma=False, semkey=None, inc=16):
        self.inc = inc
        self.eng = eng
        self.fn = fn
        self.deps = []
        self.signal = False
        self.cnt = None
        self.semkey = semkey
        self.is_dma = is_dma


class Prog:
    ENGS = ("pe", "act", "dve", "pool", "sp")

    def __init__(self, nc, same_engine_sync=("act", "dve", "pool")):
        self.nc = nc
        self.ops = {e: [] for e in self.ENGS}
        self.same = set(same_engine_sync)
        self.order = []
        self.pending = {e: [] for e in self.ENGS}
        self.last_dma = {}

    def op(self, eng, fn, reads=(), writes=(), is_dma=False, semkey=None, inc=16):
        o = Op(eng, fn, is_dma, semkey, inc)
        deps = list(self.pending[eng])
        self.pending[eng] = []
        for b in reads:
            if b.w is not None:
                deps.append(b.w)
        for b in writes:
            if b.w is not None:
                deps.append(b.w)
            deps.extend(b.r)
        for b in reads:
            if not is_dma:
                b.r = [x for x in b.r if x.is_dma or x.eng != eng]
            b.r.append(o)
        for b in writes:
            b.w = o
            b.r = []
        seen = set()
        for d in deps:
            if id(d) in seen or d is o:
                continue
            seen.add(id(d))
            if (not d.is_dma) and d.eng == eng and eng not in self.same:
                continue
            o.deps.append(d)
            d.signal = True
        self.ops[eng].append(o)
        self.order.append(o)
        if is_dma:
            self.last_dma[semkey] = o
        return o

    def dma(self, eng, out, in_, reads=(), writes=(), key=None, **kw):
        return self.op(eng, lambda e: e.dma_start(out=out, in_=in_, **kw), reads, writes,
                       is_dma=True, semkey=key)

    def barrier(self):
        lasts = []
        for e in self.ENGS:
            for o in reversed(self.ops[e]):
                if not o.is_dma:
                    lasts.append(o)
                    break
        lasts.extend(self.last_dma.values())
        for e in self.ENGS:
            self.pending[e] = list(lasts)

    def emit(self, final_wait_ops=()):
        nc = self.nc
        keys = []
        for o in self.order:
            if o.is_dma and o.semkey not in keys:
                keys.append(o.semkey)
        with contextlib.ExitStack() as st:
            sems = {}
            for e in self.ENGS:
                sems[("eng", e)] = st.enter_context(nc.semaphore("s_" + e))
            for k in keys:
                sems[("dma", k)] = st.enter_context(nc.semaphore("d_" + str(k)))
            cnts = {}
            for o in self.order:
                if o.is_dma:
                    k = ("dma", o.semkey)
                    cnts[k] = cnts.get(k, 0) + o.inc
                    o.cnt = cnts[k]
                elif o.signal:
                    k = ("eng", o.eng)
                    cnts[k] = cnts.get(k, 0) + 1
                    o.cnt = cnts[k]
            block = st.enter_context(nc.Block())
            engmap = {"pe": block.tensor, "act": block.scalar, "dve": block.vector,
                      "pool": block.gpsimd, "sp": block.sync}

            def make(ename):
                def body(eng):
                    known = {}
                    for o in self.ops[ename]:
                        need = {}
                        for d in o.deps:
                            k = ("dma", d.semkey) if d.is_dma else ("eng", d.eng)
                            if d.cnt > need.get(k, 0):
                                need[k] = d.cnt
                        for k, v in need.items():
                            if known.get(k, 0) >= v:
                                continue
                            eng.wait_ge(sems[k], v)
                            known[k] = v
                        ins = o.fn(eng)
                        if o.is_dma:
                            ins.then_inc(sems[("dma", o.semkey)], o.inc)
                        elif o.signal:
                            ins.then_inc(sems[("eng", ename)], 1)
                    if ename == "sp":
                        for o in final_wait_ops:
                            k = ("dma", o.semkey) if o.is_dma else ("eng", o.eng)
                            eng.wait_ge(sems[k], o.cnt)
                return body

            for e in self.ENGS:
                engmap[e](make(e))


class Ctx:
    def __init__(self, nc):
        self.nc = nc
        self.P = Prog(nc)
        self.uid = 0

    def sb(self, st, name, shape, dt):
        self.uid += 1
        return st.enter_context(self.nc.sbuf_tensor(f"{name}_{self.uid}", shape, dt))

    def alloc_banks(self, st):
        self.banks = []
        self.bbuf = []
        for i in range(8):
            self.banks.append(st.enter_context(self.nc.psum_tensor(f"bank{i}", [128, 512], F32)))
            self.bbuf.append(Buf(f"bank{i}"))
        self.rr = 0

    def bank4(self):
        i = self.rr % 4
        self.rr += 1
        return self.banks[i], self.bbuf[i]


class WStream:
    def __init__(self, C, st, name, ns, nk, ncols):
        self.C = C
        self.ns = ns
        self.name = name
        self.slots = [C.sb(st, f"{name}{i}", [128, nk, ncols], BF16) for i in range(ns)]
        self.bufs = [Buf(f"{name}{i}") for i in range(ns)]
        self.i = 0

    def load(self, src, nk, ncols=None, col0=0, fresh=True):
        if fresh:
            self.i += 1
        s = (self.i - 1) % self.ns
        nco = src.shape[1]
        self.C.P.dma("pool", self.slots[s][:, 0:nk, col0:col0 + nco],
                     src.rearrange("(k p) n -> p k n", p=128),
                     writes=[self.bufs[s]], key=f"{self.name}{s}")
        return self.slots[s], self.bufs[s]


def sincos(C, arg, argb, sin_t, sinb, cos_t, cosb, tmp, tmpb, ki, kib, w):
    P = C.P
    u, kf = tmp
    ub, kfb = tmpb
    P.op("dve", lambda e: e.tensor_scalar(out=u[:, :w], in0=arg[:, :w], scalar1=1.0 / TWO_PI, scalar2=None,
                                          op0=ALU.mult), reads=[argb], writes=[ub])
    P.op("dve", lambda e: e.tensor_copy(out=ki[:, :w], in_=u[:, :w]), reads=[ub], writes=[kib])
    P.op("dve", lambda e: e.tensor_copy(out=kf[:, :w], in_=ki[:, :w]), reads=[kib], writes=[kfb])
    P.op("dve", lambda e: e.scalar_tensor_tensor(out=u[:, :w], in0=kf[:, :w], scalar=-CW1, in1=arg[:, :w],
                                                 op0=ALU.mult, op1=ALU.add), reads=[kfb, argb], writes=[ub])
    P.op("dve", lambda e: e.scalar_tensor_tensor(out=u[:, :w], in0=kf[:, :w], scalar=-CW2, in1=u[:, :w],
                                                 op0=ALU.mult, op1=ALU.add), reads=[kfb, ub], writes=[ub])
    for (dst, dstb, shift) in ((sin_t, sinb, 0.0), (cos_t, cosb, math.pi / 2)):
        src = u
        if shift != 0.0:
            P.op("dve", lambda e, dst=dst: e.tensor_scalar(out=dst[:, :w], in0=u[:, :w], scalar1=shift, scalar2=None,
                                                           op0=ALU.add), reads=[ub], writes=[dstb])
            src = dst
            srcb = dstb
        else:
            srcb = ub
        P.op("dve", lambda e, src=src: e.tensor_single_scalar(out=kf[:, :w], in_=src[:, :w], scalar=math.pi, op=ALU.is_gt),
             reads=[srcb], writes=[kfb])
        P.op("dve", lambda e, src=src, dst=dst: e.scalar_tensor_tensor(out=dst[:, :w], in0=kf[:, :w], scalar=-TWO_PI,
                                                                       in1=src[:, :w], op0=ALU.mult, op1=ALU.add),
             reads=[kfb, srcb], writes=[dstb])
        P.op("dve", lambda e, dst=dst: e.tensor_single_scalar(out=kf[:, :w], in_=dst[:, :w], scalar=-math.pi, op=ALU.is_lt),
             reads=[dstb], writes=[kfb])
        P.op("dve", lambda e, dst=dst: e.scalar_tensor_tensor(out=dst[:, :w], in0=kf[:, :w], scalar=TWO_PI,
                                                              in1=dst[:, :w], op0=ALU.mult, op1=ALU.add),
             reads=[kfb, dstb], writes=[dstb])
        P.op("dve", lambda e, dst=dst: e.tensor_scalar(out=dst[:, :w], in0=dst[:, :w], scalar1=-math.pi, scalar2=math.pi,
                                                       op0=ALU.max, op1=ALU.min), reads=[dstb], writes=[dstb])
        P.op("act", lambda e, dst=dst: e.activation(out=dst[:, :w], in_=dst[:, :w], func=AF.Sin),
             reads=[dstb], writes=[dstb])


def gen_mod(C, st, w_ada_l, b_ada_l, scb, scb_buf, modT, modb, which, one11, oneb, tag):
    P = C.P
    ws = WStream(C, st, "wada" + tag, 2, 16, 512)
    brow = [C.sb(st, f"brow{i}", [1, 512], F32) for i in range(2)]
    browb = [Buf() for _ in range(2)]
    mrow = [C.sb(st, f"mrow{i}", [1, 512], F32) for i in range(2)]
    mrowb = [Buf() for _ in range(2)]
    it = 0
    for w in which:
        for half in range(4):
            col0 = w * 2048 + half * 512
            slot, sbuf = ws.load(w_ada_l[:, col0:col0 + 512], 16)
            i = it % 2
            it += 1
            P.dma("sp", brow[i][:], b_ada_l[:, col0:col0 + 512], writes=[browb[i]], key=f"brow{tag}{i}")
            bk, bkb = C.bank4()
            for k in range(16):
                P.op("pe", lambda e, k=k, slot=slot, bk=bk: e.matmul(bk[0:1, :], scb[:, k:k + 1], slot[:, k, :],
                                                                     start=(k == 0), stop=(k == 15)),
                     reads=[scb_buf, sbuf], writes=[bkb])
            P.op("dve", lambda e, i=i, bk=bk: e.tensor_tensor(out=mrow[i][:], in0=bk[0:1, :], in1=brow[i][:], op=ALU.add),
                 reads=[bkb, browb[i]], writes=[mrowb[i]])
            bk2, bk2b = C.bank4()
            for c in range(4):
                P.op("pe", lambda e, c=c, i=i, bk2=bk2: e.matmul(bk2[:, c:c + 1], mrow[i][0:1, c * 128:(c + 1) * 128],
                                                                 one11[0:1, 0:1], start=True, stop=True),
                     reads=[mrowb[i], oneb], writes=[bk2b])
            q0 = w * 16 + half * 4
            P.op("dve", lambda e, q0=q0, bk2=bk2: e.tensor_copy(out=modT[:, q0:q0 + 4], in_=bk2[:, 0:4]),
                 reads=[bk2b], writes=[modb])


def mm(P, out, lhsT, rhs, start, stop, reads, writes):
    return P.op("pe", lambda e: e.matmul(out, lhsT, rhs, start=start, stop=stop), reads=reads, writes=writes)


def act(P, out, in_, func, reads, writes, bias=None, scale=None):
    kw = {}
    if bias is not None:
        kw["bias"] = bias
    if scale is not None:
        kw["scale"] = scale
    return P.op("act", lambda e: e.activation(out=out, in_=in_, func=func, **kw), reads=reads, writes=writes)


def tt(P, eng, out, in0, in1, op, reads, writes):
    return P.op(eng, lambda e: e.tensor_tensor(out=out, in0=in0, in1=in1, op=op), reads=reads, writes=writes)


def ts(P, eng, out, in0, s1, s2, op0, op1, reads, writes):
    if s2 is None:
        return P.op(eng, lambda e: e.tensor_scalar(out=out, in0=in0, scalar1=s1, scalar2=None, op0=op0),
                    reads=reads, writes=writes)
    return P.op(eng, lambda e: e.tensor_scalar(out=out, in0=in0, scalar1=s1, scalar2=s2, op0=op0, op1=op1),
                reads=reads, writes=writes)


def stt(P, out, in0, scalar, in1, op0, op1, reads, writes):
    return P.op("dve", lambda e: e.scalar_tensor_tensor(out=out, in0=in0, scalar=scalar, in1=in1, op0=op0, op1=op1),
                reads=reads, writes=writes)


def cp(P, eng, out, in_, reads, writes):
    return P.op(eng, lambda e: e.tensor_copy(out=out, in_=in_), reads=reads, writes=writes)


def mset(P, eng, ap, val, writes):
    return P.op(eng, lambda e: e.memset(ap, val), writes=writes)


def asel(P, out, in_, pattern, cmp, fill, base, cm, reads, writes):
    return P.op("pool", lambda e: e.affine_select(out=out, in_=in_, pattern=pattern, compare_op=cmp, fill=fill,
                                                  base=base, channel_multiplier=cm), reads=reads, writes=writes)


def phase_b(C, st, dr, cfg, mods=None):
    P = C.P
    nc = C.nc
    TG = cfg["TG"]
    groups = cfg["groups"]
    NBMAX = TG // 128
    sb = lambda name, shape, dt: C.sb(st, name, shape, dt)

    ones32 = sb("ones32", [128, 128], F32); ones32b = Buf()
    mset(P, "dve", ones32[:], 1.0, [ones32b])
    ones64 = sb("ones64", [128, 64], BF16); ones64b = Buf()
    mset(P, "dve", ones64[:], 1.0, [ones64b])
    one11 = sb("one11", [1, 1], F32); one11b = Buf()
    mset(P, "dve", one11[:], 1.0, [one11b])
    mcur = sb("mcur", [128, 4, 128], BF16); mcurb = Buf()
    mprev = sb("mprev", [128, 4, 128], BF16); mprevb = Buf()
    mset(P, "pool", mcur[:], 1.0, [mcurb])
    mset(P, "pool", mprev[:], 1.0, [mprevb])
    asel(P, mcur[:], mcur[:], [[0, 4], [1, 128]], ALU.is_ge, 0.0, 0, -1, [mcurb], [mcurb])
    asel(P, mprev[:], mprev[:], [[0, 4], [-1, 128]], ALU.is_gt, 0.0, 0, 1, [mprevb], [mprevb])

    cT = sb("cT", [128, 16], F32); cTb = Buf()
    P.dma("sp", cT[:], dr["cT"], writes=[cTb], key="c0")
    scb = sb("scb", [128, 16], BF16); scbb = Buf()
    act(P, scb[:], cT[:], AF.Silu, [cTb], [scbb])
    kbias = sb("kbias", [128, 2], F32); kbiasb = Buf()
    P.dma("sp", kbias[:], dr["kbias"], writes=[kbiasb], key="c1")
    invf = sb("invf", [128, 1], F32); invfb = Buf()
    P.dma("sp", invf[:], dr["invf"], writes=[invfb], key="c2")
    perm = sb("perm", [128, 128], F32); permb = Buf()
    P.dma("sp", perm[:], dr["perm"], writes=[permb], key="c3")
    es = sb("es", [128, 32], F32); esb = Buf()
    P.dma("sp", es[:], dr["sinkb"], writes=[esb], key="c4")
    act(P, es[:], es[:], AF.Exp, [esb], [esb])
    bglu = sb("bglu", [128, 8], F32); bglub = Buf()
    P.dma("sp", bglu[:], dr["b_gluT"], writes=[bglub], key="c5")
    lnp = sb("lnp", [128, 8, 16], F32); lnpb = Buf()
    P.dma("sp", lnp[:], dr["lnp"], writes=[lnpb], key="c6")
    ts(P, "dve", lnp[:, 0:6, :], lnp[:, 0:6, :], ALPHA, None, ALU.mult, None, [lnpb], [lnpb])

    modT = [sb(f"modT{l}", [128, 96], F32) for l in range(2)]
    modb = [Buf() for _ in range(2)]
    with contextlib.ExitStack() as st2:
        gen_mod(C, st2, dr["w_ada"][0], dr["b_ada"][0], scb, scbb, modT[0], modb[0], [2, 3, 4, 5], one11, one11b, "a")
        gen_mod(C, st2, dr["w_ada"][1], dr["b_ada"][1], scb, scbb, modT[1], modb[1], [0, 1, 2, 3, 4, 5], one11, one11b, "b")
        P.barrier()
    for l in range(2):
        for w in (1, 4):
            ts(P, "dve", modT[l][:, w * 16:(w + 1) * 16], modT[l][:, w * 16:(w + 1) * 16], 1.0, 1.0 / ALPHA,
               ALU.add, ALU.mult, [modb[l]], [modb[l]])

    def mcol(l, w, m):
        return modT[l][:, w * 16 + m:w * 16 + m + 1]

    xa = sb("xa", [128, 16, TG], F32)
    hb = sb("hb", [128, 16, TG], BF16)
    gb = sb("gb", [128, 16, TG], BF16)
    NS2 = (TG + 511) // 512
    xab = [[Buf(f"xa{m}_{s}") for s in range(NS2)] for m in range(16)]
    hbb = [[Buf(f"hb{m}_{s}") for s in range(NS2)] for m in range(16)]
    gbb = [[Buf(f"gb{m}_{s}") for s in range(NS2)] for m in range(16)]
    kT2 = sb("kT2", [128, 4, TG + 128], BF16)
    kT2b = [Buf(f"k{b}") for b in range(NBMAX + 1)]
    vsb = sb("vsb", [128, NBMAX + 1, 256], BF16)
    vsbb = [Buf(f"v{b}") for b in range(NBMAX + 1)]
    cos_t = sb("cos_t", [128, TG], F32); cosb = Buf()
    sin_t = sb("sin_t", [128, TG], F32); sinb = Buf()
    posi = sb("posi", [128, 512], I32); posib = Buf()
    tmpf = [sb(f"tmpf{i}", [128, 512], F32) for i in range(2)]
    tmpfb = [Buf() for _ in range(2)]
    sq = [sb(f"sq{i}", [128, 512], F32) for i in range(2)]
    sqb = [Buf() for _ in range(2)]
    rot = [sb(f"rot{i}", [128, 512], F32) for i in range(2)]
    rotb = [Buf() for _ in range(2)]
    stm = sb("stm", [128, 512], F32); stmb = Buf()
    str_ = sb("str", [128, 512], F32); strb = Buf()
    stn = sb("stn", [128, 512], F32); stnb = Buf()
    pex = [sb(f"pex{i}", [128, 4, 128], BF16) for i in range(4)]
    pexb = [Buf() for _ in range(4)]
    den = [sb(f"den{i}", [128, 4, 128], F32) for i in range(2)]
    denb = [Buf() for _ in range(2)]
    ws = WStream(C, st, "ws", 4, 16, 128)
    cnt = {"tmpf": 0, "sq": 0, "rot": 0, "pex": 0, "den": 0}

    def nxt(name, n):
        i = cnt[name] % n
        cnt[name] += 1
        return i

    xT_v = dr[cfg.get("xb_name", "xT")].rearrange("(k p) t -> p k t", p=128)
    amT_v = dr["amT"].rearrange("c p t -> p c t") if "amT" in dr else None
    yT_v = dr["yT"].rearrange("(k p) t -> p k t", p=128)
    out_ops = []

    def resid(bank, bankb, m, si, off, w, gate_col, stats, first, last):
        stt(P, xa[:, m, off:off + w], bank[:, :w], gate_col, xa[:, m, off:off + w], ALU.mult, ALU.add,
            [bankb, xab[m][si]], [xab[m][si]])
        if stats:
            s1, s1b = C.banks[4 + 2 * si], C.bbuf[4 + 2 * si]
            s2, s2b = C.banks[5 + 2 * si], C.bbuf[5 + 2 * si]
            mm(P, s1[:, :w], ones32[:], xa[:, m, off:off + w], first, last, [ones32b, xab[m][si]], [s1b])
            i = nxt("sq", 2)
            act(P, sq[i][:, :w], xa[:, m, off:off + w], AF.Square, [xab[m][si]], [sqb[i]])
            mm(P, s2[:, :w], ones32[:], sq[i][:, :w], first, last, [ones32b, sqb[i]], [s2b])

    def ln_finish(si, off, w, gi_, bi_, A_l, A_w, sh_w, final):
        s1, s1b = C.banks[4 + 2 * si], C.bbuf[4 + 2 * si]
        s2, s2b = C.banks[5 + 2 * si], C.bbuf[5 + 2 * si]
        ts(P, "dve", stm[:, :w], s1[:, :w], 1.0 / D, None, ALU.mult, None, [s1b], [stmb])
        tt(P, "dve", stn[:, :w], stm[:, :w], stm[:, :w], ALU.mult, [stmb], [stnb])
        stt(P, str_[:, :w], s2[:, :w], 1.0 / D, stn[:, :w], ALU.mult, ALU.subtract, [s2b, stnb], [strb])
        ts(P, "dve", str_[:, :w], str_[:, :w], LN_EPS, None, ALU.add, None, [strb], [strb])
        act(P, str_[:, :w], str_[:, :w], AF.Sqrt, [strb], [strb])
        P.op("dve", lambda e: e.reciprocal(out=str_[:, :w], in_=str_[:, :w]), reads=[strb], writes=[strb])
        stt(P, stn[:, :w], stm[:, :w], -1.0, str_[:, :w], ALU.mult, ALU.mult, [stmb, strb], [stnb])
        for m in range(16):
            xs = xa[:, m, off:off + w]
            tt(P, "dve", xs, xs, str_[:, :w], ALU.mult, [xab[m][si], strb], [xab[m][si]])
            tt(P, "dve", xs, xs, stn[:, :w], ALU.add, [xab[m][si], stnb], [xab[m][si]])
            ts(P, "dve", xs, xs, lnp[:, gi_, m:m + 1], lnp[:, bi_, m:m + 1], ALU.mult, ALU.add,
               [xab[m][si], lnpb], [xab[m][si]])
            if not final:
                act(P, hb[:, m, off:off + w], xs, AF.Identity, [xab[m][si], modb[A_l]], [hbb[m][si]],
                    bias=mcol(A_l, sh_w, m), scale=mcol(A_l, A_w, m))

    def ffn(l, subs, gate_w, ln_g, ln_b, nxt_mod, final):
        wg_l, wu_l, wd_l = dr["wg"][l], dr["wu"][l], dr["wd"][l]
        NPASS, JP = 4, 11
        for hp in range(NPASS):
            for jj in range(JP):
                j = hp * JP + jj
                wgs, wgb = ws.load(wg_l[:, j * 128:(j + 1) * 128], 16)
                wus, wub = ws.load(wu_l[:, j * 128:(j + 1) * 128], 16)
                for si, (off, w) in enumerate(subs):
                    bg, bgb = C.bank4()
                    bu, bub = C.bank4()
                    for k in range(16):
                        mm(P, bg[:, :w], wgs[:, k, :], hb[:, k, off:off + w], k == 0, k == 15, [wgb, hbb[k][si]], [bgb])
                    for k in range(16):
                        mm(P, bu[:, :w], wus[:, k, :], hb[:, k, off:off + w], k == 0, k == 15, [wub, hbb[k][si]], [bub])
                    i = nxt("tmpf", 2)
                    act(P, tmpf[i][:, :w], bg[:, :w], AF.Silu, [bgb], [tmpfb[i]])
                    tt(P, "dve", gb[:, jj, off:off + w], tmpf[i][:, :w], bu[:, :w], ALU.mult, [tmpfb[i], bub], [gbb[jj][si]])
            for m in range(16):
                wds, wdb = ws.load(wd_l[hp * JP * 128:(hp + 1) * JP * 128, m * 128:(m + 1) * 128], JP)
                for si, (off, w) in enumerate(subs):
                    bk, bkb = C.bank4()
                    for jj in range(JP):
                        mm(P, bk[:, :w], wds[:, jj, :], gb[:, jj, off:off + w], jj == 0, jj == JP - 1, [wdb, gbb[jj][si]], [bkb])
                    resid(bk, bkb, m, si, off, w, mcol(l, gate_w, m), hp == NPASS - 1, m == 0, m == 15)
        for si, (off, w) in enumerate(subs):
            ln_finish(si, off, w, ln_g, ln_b, nxt_mod[0], nxt_mod[1], nxt_mod[2], final)

    first_main = True
    for gi, g in enumerate(groups):
        t0, tg, halo = g["t0"], g["tg"], g["halo"]
        subs = [(o, min(512, tg - o)) for o in range(0, tg, 512)]
        nblk = tg // 128
        allx = [xab[m][si] for m in range(16) for si in range(len(subs))]
        P.dma("sp", xa[:, :, 0:tg], xT_v[:, :, t0:t0 + tg], writes=allx, key="xin")
        if "am_loader" in cfg:
            cfg["am_loader"](P, gb, t0, tg, lambda h: [gbb[8 * h + m][si] for m in range(8) for si in range(len(subs))])
        else:
            P.dma("sp", gb[:, 0:16, 0:tg], amT_v[:, :, t0:t0 + tg],
                  writes=[gbb[m][si] for m in range(16) for si in range(len(subs))], key="ain")
        for m in range(16):
            for si, (off, w) in enumerate(subs):
                act(P, xa[:, m, off:off + w], xa[:, m, off:off + w], AF.Identity, [xab[m][si]], [xab[m][si]], scale=ALPHA)
        for (off, w) in subs:
            P.dma("sp", posi[:, 0:w], dr["pos"][:, t0 + off:t0 + off + w].partition_broadcast(128), writes=[posib], key="pin")
            i = nxt("rot", 2)
            cp(P, "dve", rot[i][:, :w], posi[:, 0:w], [posib], [rotb[i]])
            ts(P, "dve", rot[i][:, :w], rot[i][:, :w], invf[:, 0:1], None, ALU.mult, None, [rotb[i], invfb], [rotb[i]])
            sincos(C, rot[i], rotb[i], sin_t[:, off:off + w], sinb, cos_t[:, off:off + w], cosb,
                   [tmpf[0], tmpf[1]], [tmpfb[0], tmpfb[1]], sq[0][:].bitcast(I32), sqb[0], w)

        wglu = dr["w_glu"]
        for m in range(8):
            wsl, wsb = ws.load(wglu[:, m * 128:(m + 1) * 128], 8)
            for si, (off, w) in enumerate(subs):
                bk, bkb = C.bank4()
                for k in range(8):
                    mm(P, bk[:, :w], wsl[:, k, :], gb[:, 8 + k, off:off + w], k == 0, k == 7, [wsb, gbb[8 + k][si]], [bkb])
                i = nxt("tmpf", 2)
                act(P, tmpf[i][:, :w], bk[:, :w], AF.Sigmoid, [bkb, bglub], [tmpfb[i]], bias=bglu[:, m:m + 1])
                tt(P, "dve", hb[:, m, off:off + w], gb[:, 8 + m, off:off + w], tmpf[i][:, :w], ALU.mult,
                   [gbb[8 + m][si], tmpfb[i]], [hbb[m][si]])
        wo = dr["w_out_ab"]
        for m in range(16):
            wsl, wsb = ws.load(wo[:, m * 128:(m + 1) * 128], 16)
            for si, (off, w) in enumerate(subs):
                bk, bkb = C.bank4()
                for k in range(8):
                    mm(P, bk[:, :w], wsl[:, k, :], gb[:, k, off:off + w], k == 0, False, [wsb, gbb[k][si]], [bkb])
                for k in range(8):
                    mm(P, bk[:, :w], wsl[:, 8 + k, :], hb[:, k, off:off + w], False, k == 7, [wsb, hbb[k][si]], [bkb])
                resid(bk, bkb, m, si, off, w, mcol(0, 2, m), True, m == 0, m == 15)
        for si, (off, w) in enumerate(subs):
            ln_finish(si, off, w, 0, 1, 0, 4, 3, False)
        ffn(0, subs, 5, 2, 3, (1, 1, 0), False)

        wi = dr["w_in_c"]
        def proj_rot(wsl, wsb, dst_fn, dst_bufs_fn):
            for si, (off, w) in enumerate(subs):
                bk, bkb = C.bank4()
                for k in range(16):
                    mm(P, bk[:, :w], wsl[:, k, :], hb[:, k, off:off + w], k == 0, k == 15, [wsb, hbb[k][si]], [bkb])
                i = nxt("tmpf", 2)
                act(P, tmpf[i][:, :w], bk[:, :w], AF.Identity, [bkb], [tmpfb[i]])
                b2, b2b = C.bank4()
                mm(P, b2[:, :w], perm[:], tmpf[i][:, :w], True, True, [permb, tmpfb[i]], [b2b])
                r1 = nxt("rot", 2)
                tt(P, "dve", rot[r1][:, :w], tmpf[i][:, :w], cos_t[:, off:off + w], ALU.mult, [tmpfb[i], cosb], [rotb[r1]])
                r2 = nxt("rot", 2)
                tt(P, "dve", rot[r2][:, :w], b2[:, :w], sin_t[:, off:off + w], ALU.mult, [b2b, sinb], [rotb[r2]])
                tt(P, "dve", dst_fn(off, w), rot[r1][:, :w], rot[r2][:, :w], ALU.add, [rotb[r1], rotb[r2]],
                   dst_bufs_fn(si, off, w))

        for kv in range(4):
            c0 = 2048 + kv * 64
            ws.load(wi[:, c0:c0 + 64], 16, col0=0)
            wsl, wsb = ws.load(wi[:, c0:c0 + 64], 16, col0=64, fresh=False)
            proj_rot(wsl, wsb, lambda off, w, kv=kv: kT2[:, kv, 128 + off:128 + off + w],
                     lambda si, off, w: [kT2b[1 + (off + o) // 128] for o in range(0, w, 128)])
        wv0, wv0b = ws.load(wi[:, 2304:2432], 16)
        wv1, wv1b = ws.load(wi[:, 2432:2560], 16)
        for tb in range(nblk):
            si = tb // 4
            bk, bkb = C.bank4()
            for (wvl, wvb, cc) in ((wv0, wv0b, 0), (wv1, wv1b, 128)):
                for k in range(16):
                    mm(P, bk[:, cc:cc + 128], hb[:, k, tb * 128:(tb + 1) * 128], wvl[:, k, :], k == 0, k == 15,
                       [hbb[k][si], wvb], [bkb])
            act(P, vsb[:, 1 + tb, :], bk[:, 0:256], AF.Identity, [bkb], [vsbb[1 + tb]])
        if not halo:
            for m in range(16):
                wsl, wsb = ws.load(wi[:, m * 128:(m + 1) * 128], 16)
                proj_rot(wsl, wsb, lambda off, w, m=m: gb[:, m, off:off + w],
                         lambda si, off, w, m=m: [gbb[m][si]])
            par = 0
            for b in range(nblk):
                si = b // 4
                kcol = 0 if (first_main and b == 0) else 1
                for kv in range(4):
                    for hh in range(2):
                        ps_ = slice(hh * 64, hh * 64 + 64)
                        base = 4 * (par % 2)
                        par += 1
                        bS = [C.banks[base + 0], C.banks[base + 1]]
                        bSb = [C.bbuf[base + 0], C.bbuf[base + 1]]
                        bO, bOb = C.banks[base + 2], C.bbuf[base + 2]
                        bZ, bZb = C.banks[base + 3], C.bbuf[base + 3]
                        qv = gb[ps_, 4 * kv:4 * kv + 4, b * 128:(b + 1) * 128]
                        qbufs = [gbb[4 * kv + c][si] for c in range(4)]
                        pis = []
                        for which in range(2):
                            kb = b + which
                            mm(P, bS[which][:, :], kT2[ps_, kv, kb * 128:(kb + 1) * 128], qv, True, True,
                               [kT2b[kb]] + qbufs, [bSb[which]])
                            pi = nxt("pex", 4)
                            pis.append(pi)
                            bcol = kcol if which == 0 else 1
                            act(P, pex[pi][:].rearrange("p a b -> p (a b)"), bS[which][:, :], AF.Exp,
                                [bSb[which], kbiasb], [pexb[pi]], bias=kbias[:, bcol:bcol + 1], scale=0.125)
                            msk, mskb = (mprev, mprevb) if which == 0 else (mcur, mcurb)
                            tt(P, "dve", pex[pi][:], pex[pi][:], msk[:], ALU.mult, [pexb[pi], mskb], [pexb[pi]])
                        for which in range(2):
                            kb = b + which
                            pv = pex[pis[which]][:].rearrange("p a b -> p (a b)")
                            mm(P, bO[ps_, :], vsb[:, kb, kv * 64:(kv + 1) * 64], pv, which == 0, which == 1,
                               [vsbb[kb], pexb[pis[which]]], [bOb])
                        for which in range(2):
                            pv = pex[pis[which]][:].rearrange("p a b -> p (a b)")
                            mm(P, bZ[ps_, :], ones64[:], pv, which == 0, which == 1,
                               [ones64b, pexb[pis[which]]], [bZb])
                        di = nxt("den", 2)
                        dv = den[di][ps_]
                        tt(P, "dve", dv, bZ[ps_, :].rearrange("p (a b) -> p a b", a=4),
                           es[ps_, 8 * kv + 4 * hh:8 * kv + 4 * hh + 4].unsqueeze(2).broadcast_to([64, 4, 128]), ALU.add,
                           [bZb, esb], [denb[di]])
                        P.op("dve", lambda e, dv=dv: e.reciprocal(out=dv, in_=dv), reads=[denb[di]], writes=[denb[di]])
                        tt(P, "dve", hb[ps_, 4 * kv:4 * kv + 4, b * 128:(b + 1) * 128],
                           bO[ps_, :].rearrange("p (a b) -> p a b", a=4), dv, ALU.mult,
                           [bOb, denb[di]], [hbb[4 * kv + c][si] for c in range(4)])
            first_main = False
        if gi + 1 < len(groups):
            cp(P, "dve", kT2[:, :, 0:128], kT2[:, :, tg:tg + 128], [kT2b[nblk]], [kT2b[0]])
            cp(P, "dve", vsb[:, 0, :], vsb[:, nblk, :], [vsbb[nblk]], [vsbb[0]])
        if halo:
            continue
        wo = dr["w_out_c"]
        for m in range(16):
            wsl, wsb = ws.load(wo[:, m * 128:(m + 1) * 128], 16)
            for si, (off, w) in enumerate(subs):
                bk, bkb = C.bank4()
                for k in range(16):
                    mm(P, bk[:, :w], wsl[:, k, :], hb[:, k, off:off + w], k == 0, k == 15, [wsb, hbb[k][si]], [bkb])
                resid(bk, bkb, m, si, off, w, mcol(1, 2, m), True, m == 0, m == 15)
        for si, (off, w) in enumerate(subs):
            ln_finish(si, off, w, 4, 5, 1, 4, 3, False)
        ffn(1, subs, 5, 6, 7, (1, 1, 0), True)
        o = P.dma("sp", yT_v[:, :, t0 - HALO:t0 - HALO + tg], xa[:, :, 0:tg], reads=allx, key="yout")
        out_ops.append(o)
    return out_ops


def make_cfg(L, **kw):
    ntok = L // NCORE
    NT = ntok + HALO
    TG = min(1024, ntok)
    groups = [dict(t0=0, tg=HALO, halo=True)]
    for t in range(HALO, NT, TG):
        groups.append(dict(t0=t, tg=min(TG, NT - t), halo=False))
    d = dict(L=L, NTOK=ntok, NT=NT, TG=TG, groups=groups)
    d.update(kw)
    return d


def fm(v):
    return np.ascontiguousarray(np.asarray(v, np.float32).reshape(-1, 128).T)


def const_perm():
    p = np.zeros((128, 128), np.float32)
    for d in range(128):
        r = d % 64
        if r < 8:
            p[d + 8, d] = -1.0
        elif r < 16:
            p[d - 8, d] = 1.0
    return p


def const_invf():
    v = np.zeros((128, 1), np.float32)
    inv = np.power(np.float32(500000.0), -np.arange(8, dtype=np.float32) * np.float32(2.0 / 16)).astype(np.float32)
    for d in range(128):
        r = d % 64
        if r < 16:
            v[d, 0] = inv[r % 8]
    return v


B_INPUTS = [
    ("xT", lambda c: [2048, c["NT"]], F32), ("amT", lambda c: [16, 128, c["NT"]], BF16),
    ("cT", lambda c: [128, 16], F32), ("pos", lambda c: [1, c["NT"]], I32), ("kbias", lambda c: [128, 2], F32),
    ("invf", lambda c: [128, 1], F32), ("perm", lambda c: [128, 128], F32), ("sinkb", lambda c: [128, 32], F32),
    ("b_gluT", lambda c: [128, 8], F32), ("lnp", lambda c: [128, 8, 16], F32),
    ("w_glu", lambda c: [1024, 1024], F32), ("w_out_ab", lambda c: [2048, 2048], F32),
    ("w_in_c", lambda c: [2048, 2560], F32), ("w_out_c", lambda c: [2048, 2048], F32),
    ("w_ada", lambda c: [2, 2048, 12288], F32), ("b_ada", lambda c: [2, 1, 12288], F32),
    ("wg", lambda c: [2, 2048, DFF], F32), ("wu", lambda c: [2, 2048, DFF], F32), ("wd", lambda c: [2, DFF, 2048], F32),
]


def build_b(cfg):
    nc = bass.Bass("TRN2", target_bir_lowering=False)
    dr = {}
    for name, shp, dt in B_INPUTS:
        dr[name] = nc.dram_tensor(name, shp(cfg), dt, kind="ExternalInput").ap()
    dr["yT"] = nc.dram_tensor("yT", [2048, cfg["NTOK"]], F32, kind="ExternalOutput").ap()
    C = Ctx(nc)
    with contextlib.ExitStack() as st:
        C.alloc_banks(st)
        outs = phase_b(C, st, dr, cfg)
        C.P.emit(final_wait_ops=outs)
    return nc


def prep_b(inp, amT_full, cfg):
    L, NTOK, NT = cfg["L"], cfg["NTOK"], cfg["NT"]
    x = np.asarray(inp["x"], np.float32)[0]
    pos = np.asarray(inp["positions"], np.int32)[0]
    sinks = np.asarray(inp["attn_sinks"], np.float32)[0]
    sk = np.zeros(32, np.float32)
    for kv in range(4):
        for hh in range(2):
            for c in range(4):
                sk[kv * 8 + hh * 4 + c] = sinks[8 * kv + 2 * c + hh]
    lnp = np.stack([fm(inp["ln_mix_g"][0]), fm(inp["ln_mix_b"][0]), fm(inp["ln_ffn_g"][0]), fm(inp["ln_ffn_b"][0]),
                    fm(inp["ln_mix_g"][1]), fm(inp["ln_mix_b"][1]), fm(inp["ln_ffn_g"][1]), fm(inp["ln_ffn_b"][1])], axis=1)
    shared = dict(
        cT=fm(inp["c"][0]), invf=const_invf(), perm=const_perm(),
        sinkb=np.ascontiguousarray(np.broadcast_to(sk[None, :], (128, 32))),
        b_gluT=np.ascontiguousarray(np.asarray(inp["b_glu"], np.float32)[0].reshape(8, 128).T),
        lnp=np.ascontiguousarray(lnp),
        w_glu=np.asarray(inp["w_glu"], np.float32)[0], w_out_ab=np.asarray(inp["w_out_ab"], np.float32)[0],
        w_in_c=np.asarray(inp["w_in_c"], np.float32)[0], w_out_c=np.asarray(inp["w_out_c"], np.float32)[0],
        w_ada=np.asarray(inp["w_ada"], np.float32), b_ada=np.asarray(inp["b_ada"], np.float32)[:, None, :],
        wg=np.asarray(inp["w_ffn_gate"], np.float32), wu=np.asarray(inp["w_ffn_up"], np.float32),
        wd=np.asarray(inp["w_ffn_down"], np.float32),
    )
    maps = []
    for j in range(NCORE):
        lo = j * NTOK - HALO
        xT = np.zeros((2048, NT), np.float32)
        am = np.zeros((16, 128, NT), ml_dtypes.bfloat16)
        ps = np.zeros((1, NT), np.int32)
        s0 = max(lo, 0)
        xT[:, s0 - lo:] = x[s0:lo + NT].T
        if amT_full is not None:
            am[:, :, s0 - lo:] = amT_full[:, :, s0:lo + NT]
        ps[0, s0 - lo:] = pos[s0:lo + NT]
        kb = np.zeros((128, 2), np.float32)
        if j == 0:
            kb[:, 0] = -30000.0
        m = dict(shared)
        m.update(xT=xT, amT=am, pos=ps, kbias=kb)
        maps.append(m)
    return maps


def gather_b(res, cfg):
    outs = [np.asarray(r["yT"]).T for r in res.results]
    return np.concatenate(outs, axis=0)[None].astype(np.float32)


A_INPUTS = [
    ("xT", lambda c: [2048, c["L"]], F32), ("cT", lambda c: [128, 16], F32),
    ("w_inA", lambda c: [2048, 513], F32), ("w_ada", lambda c: [2, 2048, 12288], F32), ("b_ada", lambda c: [2, 1, 12288], F32),
    ("bf", lambda c: [128, 1], F32), ("ssm_s", lambda c: [128, 3, 8], F32),
    ("ssm_b", lambda c: [128, 2, 8, 16], F32), ("ssm_c", lambda c: [128, 2, 8, 16], F32),
    ("dcol", lambda c: [128, 1], F32), ("hsel", lambda c: [128, 5], F32), ("gmask", lambda c: [128, 8], F32),
    ("iota_t", lambda c: [128, 512], F32), ("swapI", lambda c: [128, 128], F32), ("ident", lambda c: [128, 128], F32),
]


def phase_a(C, st, dr, cfg):
    P = C.P
    L = cfg["L"]
    NBLK = L // 128
    NQ = L // 512
    PCH = 256
    sb = lambda name, shape, dt: C.sb(st, name, shape, dt)

    qT = sb("qT", [128, L], BF16)
    kT = sb("kT", [128, L], BF16)
    uT = sb("uT", [128, L], BF16)
    vtok = sb("vtok", [128, NBLK, 128], BF16)
    flog = sb("flog", [128, NBLK], F32)
    qTb = [Buf() for _ in range(L // PCH)]
    kTb = [Buf() for _ in range(L // PCH)]
    uTb = [Buf() for _ in range(L // PCH)]
    vtb = [Buf() for _ in range(L // PCH)]
    flogb = Buf()

    def pcb(bufs, t0, t1):
        return [bufs[i] for i in range(t0 // PCH, (t1 + PCH - 1) // PCH)]

    ones32 = sb("ones32", [128, 128], F32); ones32b = Buf()
    mset(P, "dve", ones32[:], 1.0, [ones32b])
    one11 = sb("one11", [1, 1], F32); one11b = Buf()
    mset(P, "dve", one11[:], 1.0, [one11b])
    mdiag = sb("mdiag", [128, 128], BF16); mdiagb = Buf()
    mset(P, "pool", mdiag[:], 1.0, [mdiagb])
    asel(P, mdiag[:], mdiag[:], [[1, 128]], ALU.is_ge, 0.0, 0, -1, [mdiagb], [mdiagb])

    def ld(name, shape, dt=F32):
        t = sb(name, shape, dt)
        b = Buf()
        P.dma("sp", t[:], dr[name], writes=[b], key="ca_" + name)
        return t, b

    cT, cTb = ld("cT", [128, 16])
    bfc, bfcb = ld("bf", [128, 1])
    ts(P, "dve", bfc[:], bfc[:], -1.0, None, ALU.mult, None, [bfcb], [bfcb])
    dcol, dcolb = ld("dcol", [128, 1])
    hsel, hselb = ld("hsel", [128, 5])
    gmask, gmaskb = ld("gmask", [128, 8])
    ident, identb = ld("ident", [128, 128])
    scb = sb("scb", [128, 16], BF16); scbb = Buf()
    act(P, scb[:], cT[:], AF.Silu, [cTb], [scbb])
    modT = sb("modTa", [128, 32], F32); modb = Buf()

    with contextlib.ExitStack() as st1:
        with contextlib.ExitStack() as st2:
            gen_mod(C, st2, dr["w_ada"][0], dr["b_ada"][0], scb, scbb, modT, modb, [0, 1], one11, one11b, "A")
            P.barrier()
        ts(P, "dve", modT[:, 16:32], modT[:, 16:32], 1.0, None, ALU.add, None, [modb], [modb])
        win = C.sb(st1, "win", [128, 16, 513], BF16); winb = Buf()
        if "zpad_ap" in cfg:
            zt = C.sb(st1, "zt", [128, HALO], BF16); ztb = Buf()
            mset(P, "dve", zt[:], 0.0, [ztb])
            for zi, zap in enumerate(cfg["zpad_ap"]):
                zb = Buf()
                cfg.setdefault("a_out_bufs", []).append(zb)
                P.dma("sp", zap, zt[:], reads=[ztb], writes=[zb], key=f"zpad{zi}")
        P.dma("pool", win[:], dr["w_inA"].rearrange("(k p) n -> p k n", p=128), writes=[winb], key="win")
        NXF = 2
        xf = [C.sb(st1, f"xf{i}", [128, 16, PCH], F32) for i in range(NXF)]
        xfb = [[Buf() for _ in range(4)] for _ in range(NXF)]
        hT = [C.sb(st1, f"hT{i}", [128, 16, PCH], BF16) for i in range(2)]
        hTb = [[Buf() for _ in range(16)] for _ in range(2)]
        xT_v = dr["xT"].rearrange("(k p) t -> p k t", p=128)
        for ch in range(L // PCH):
            i = ch % 2
            xi = ch % NXF
            t0 = ch * PCH
            for qd in range(4):
                P.dma("sp", xf[xi][:, 4 * qd:4 * qd + 4, :], xT_v[:, 4 * qd:4 * qd + 4, t0:t0 + PCH], writes=[xfb[xi][qd]],
                      key=f"xf{xi}_{qd}")
            for k in range(16):
                if k % 2 == 0:
                    act(P, hT[i][:, k, :], xf[xi][:, k, :], AF.Identity, [xfb[xi][k // 4], modb], [hTb[i][k]],
                        bias=modT[:, k:k + 1], scale=modT[:, 16 + k:17 + k])
                else:
                    ts(P, "dve", hT[i][:, k, :], xf[xi][:, k, :], modT[:, 16 + k:17 + k], modT[:, k:k + 1], ALU.mult, ALU.add,
                       [xfb[xi][k // 4], modb], [hTb[i][k]])
            for (dst, dbufs, c0, eng) in ((qT, qTb, 0, "act"), (kT, kTb, 128, "dve"), (uT, uTb, 256, "act")):
                bk, bkb = C.bank4()
                for k in range(16):
                    mm(P, bk[:, :PCH], win[:, k, c0:c0 + 128], hT[i][:, k, :], k == 0, k == 15, [winb, hTb[i][k]], [bkb])
                if eng == "act":
                    act(P, dst[:, t0:t0 + PCH], bk[:, :PCH], AF.Identity, [bkb], [dbufs[ch]])
                else:
                    cp(P, "dve", dst[:, t0:t0 + PCH], bk[:, :PCH], [bkb], [dbufs[ch]])
            for tb in range(PCH // 128):
                bk, bkb = C.bank4()
                for k in range(16):
                    mm(P, bk[:, 0:129], hT[i][:, k, tb * 128:(tb + 1) * 128], win[:, k, 384:513], k == 0, k == 15,
                       [winb, hTb[i][k]], [bkb])
                blk = t0 // 128 + tb
                cp(P, "dve", vtok[:, blk, :], bk[:, 0:128], [bkb], [vtb[ch]])
                cp(P, "dve", flog[:, blk:blk + 1], bk[:, 128:129], [bkb], [flogb])
        P.barrier()

    sb_main = sb
    pre = {}
    for (nm, shp, dt_) in (("Fsb", [128, NBLK], F32), ("Fend", [128, NQ], F32), ("sm", [128, 16, 8], F32),
                           ("Bm0", [128, 8, 128], BF16), ("Bm1", [128, 8, 128], BF16), ("Cz0", [128, 8, 128], BF16),
                           ("Cz1", [128, 8, 128], BF16), ("Dm", [128, 128], BF16), ("Rm", [128, 8, 128], F32),
                           ("Tc", [128, 8, 512], F32), ("Ts", [128, 8, 512], F32)):
        pre[nm] = sb_main(nm, shp, dt_)
    stS = contextlib.ExitStack()
    sb = lambda name, shape, dt: pre[name] if name in pre else C.sb(stS, name, shape, dt)
    uneg = sb("uneg", [128, 128], F32); unegb = Buf()
    mset(P, "pool", uneg[:], -1.0, [unegb])
    asel(P, uneg[:], uneg[:], [[1, 128]], ALU.is_ge, 0.0, 0, -1, [unegb], [unegb])
    suneg = sb("suneg", [128, 128], F32); sunegb = Buf()
    mset(P, "pool", suneg[:], -1.0, [sunegb])
    asel(P, suneg[:], suneg[:], [[1, 128]], ALU.is_gt, 0.0, 0, -1, [sunegb], [sunegb])
    e127 = sb("e127", [128, 128], F32); e127b = Buf()
    mset(P, "pool", e127[:], 1.0, [e127b])
    asel(P, e127[:], e127[:], [[0, 128]], ALU.is_ge, 0.0, -127, 1, [e127b], [e127b])
    ssm_s, ssm_sb = ld("ssm_s", [128, 3, 8])
    ssm_b, ssm_bb = ld("ssm_b", [128, 2, 8, 16])
    ssm_c, ssm_cb = ld("ssm_c", [128, 2, 8, 16])
    iota_t, iotab = ld("iota_t", [128, 512])
    swapI, swapIb = ld("swapI", [128, 128])
    lsm = sb("lsm", [128, NBLK], F32); lsmb = Buf()
    act(P, lsm[:], flog[:], AF.Exp, [flogb, bfcb], [lsmb], bias=bfc[:, 0:1], scale=-1.0)
    ts(P, "dve", lsm[:], lsm[:], 1.0, None, ALU.add, None, [lsmb], [lsmb])
    act(P, lsm[:], lsm[:], AF.Ln, [lsmb], [lsmb])
    Fb, Fbb = C.banks[4], C.bbuf[4]
    Tb_, Tbb = C.banks[5], C.bbuf[5]
    mm(P, Tb_[0:NBLK, 0:128], lsm[:, 0:NBLK], ones32[:], True, True, [lsmb, ones32b], [Tbb])
    totb = sb("totb", [128, 128], F32); totbb = Buf()
    cp(P, "dve", totb[0:NBLK, :], Tb_[0:NBLK, 0:128], [Tbb], [totbb])
    mm(P, Fb[:, 0:NBLK], uneg[:], lsm[:, 0:NBLK], True, False, [unegb, lsmb], [Fbb])
    mm(P, Fb[:, 0:NBLK], totb[0:NBLK, :], suneg[0:NBLK, 0:NBLK], False, True, [totbb, sunegb], [Fbb])
    Fsb = sb("Fsb", [128, NBLK], F32); Fsbb = Buf()
    cp(P, "dve", Fsb[:], Fb[:, 0:NBLK], [Fbb], [Fsbb])
    mm(P, Tb_[:, 0:NQ], e127[:], Fsb[:].rearrange("p (g f) -> p g f", f=4)[:, :, 3], True, True, [e127b, Fsbb], [Tbb])
    Fend = sb("Fend", [128, NQ], F32); Fendb = Buf()
    cp(P, "dve", Fend[:], Tb_[:, 0:NQ], [Tbb], [Fendb])

    lamre, lamim, ldt = ssm_s[:, 0, :], ssm_s[:, 1, :], ssm_s[:, 2, :]
    sm = sb("sm", [128, 16, 8], F32)
    smb = [Buf() for _ in range(16)]
    DT, ARE, TH, RHO, SN, CS, LBR, LBI, RDEN, NR, QRE, QIM, QA, QB, QA2, TMP = range(16)
    V = lambda i: sm[:, i, :]
    act(P, V(DT), ldt, AF.Exp, [ssm_sb], [smb[DT]])
    tt(P, "dve", V(ARE), lamre, V(DT), ALU.mult, [ssm_sb, smb[DT]], [smb[ARE]])
    tt(P, "dve", V(TH), lamim, V(DT), ALU.mult, [ssm_sb, smb[DT]], [smb[TH]])
    act(P, V(RHO), V(ARE), AF.Exp, [smb[ARE]], [smb[RHO]])
    sc_tmp = [sb(f"sct{i}", [128, 8], F32) for i in range(2)]
    sc_tmpb = [Buf() for _ in range(2)]
    sc_ki = sb("scki", [128, 8], I32); sc_kib = Buf()
    sincos(C, V(TH), smb[TH], V(SN), smb[SN], V(CS), smb[CS], sc_tmp, sc_tmpb, sc_ki, sc_kib, 8)
    tt(P, "dve", V(LBR), V(RHO), V(CS), ALU.mult, [smb[RHO], smb[CS]], [smb[LBR]])
    tt(P, "dve", V(LBI), V(RHO), V(SN), ALU.mult, [smb[RHO], smb[SN]], [smb[LBI]])
    tt(P, "dve", V(RDEN), lamre, lamre, ALU.mult, [ssm_sb], [smb[RDEN]])
    tt(P, "dve", V(TMP), lamim, lamim, ALU.mult, [ssm_sb], [smb[TMP]])
    tt(P, "dve", V(RDEN), V(RDEN), V(TMP), ALU.add, [smb[RDEN], smb[TMP]], [smb[RDEN]])
    P.op("dve", lambda e: e.reciprocal(out=V(RDEN), in_=V(RDEN)), reads=[smb[RDEN]], writes=[smb[RDEN]])
    ts(P, "dve", V(NR), V(LBR), -1.0, None, ALU.add, None, [smb[LBR]], [smb[NR]])
    tt(P, "dve", V(QRE), V(NR), lamre, ALU.mult, [smb[NR], ssm_sb], [smb[QRE]])
    tt(P, "dve", V(TMP), V(LBI), lamim, ALU.mult, [smb[LBI], ssm_sb], [smb[TMP]])
    tt(P, "dve", V(QRE), V(QRE), V(TMP), ALU.add, [smb[QRE], smb[TMP]], [smb[QRE]])
    tt(P, "dve", V(QRE), V(QRE), V(RDEN), ALU.mult, [smb[QRE], smb[RDEN]], [smb[QRE]])
    tt(P, "dve", V(QIM), V(LBI), lamre, ALU.mult, [smb[LBI], ssm_sb], [smb[QIM]])
    tt(P, "dve", V(TMP), V(NR), lamim, ALU.mult, [smb[NR], ssm_sb], [smb[TMP]])
    tt(P, "dve", V(QIM), V(QIM), V(TMP), ALU.subtract, [smb[QIM], smb[TMP]], [smb[QIM]])
    tt(P, "dve", V(QIM), V(QIM), V(RDEN), ALU.mult, [smb[QIM], smb[RDEN]], [smb[QIM]])
    top, bot, ntop, nbot, tmb = (hsel[:, i:i + 1] for i in range(5))
    ts(P, "dve", V(TMP), V(QIM), bot, None, ALU.mult, None, [smb[QIM], hselb], [smb[TMP]])
    stt(P, V(QA), V(QRE), top, V(TMP), ALU.mult, ALU.add, [smb[QRE], smb[TMP], hselb], [smb[QA]])
    ts(P, "dve", V(TMP), V(QRE), bot, None, ALU.mult, None, [smb[QRE], hselb], [smb[TMP]])
    stt(P, V(QB), V(QIM), ntop, V(TMP), ALU.mult, ALU.add, [smb[QIM], smb[TMP], hselb], [smb[QB]])
    stt(P, V(QA2), V(QIM), top, V(TMP), ALU.mult, ALU.subtract, [smb[QIM], smb[TMP], hselb], [smb[QA2]])
    bre2, bim2 = ssm_b[:, 0], ssm_b[:, 1]
    cre2, cim2 = ssm_c[:, 0], ssm_c[:, 1]
    bc = lambda i: V(i).unsqueeze(2).broadcast_to([128, 8, 16])
    M1 = sb("M1", [128, 8, 16], F32); M1b = Buf()
    M2 = sb("M2", [128, 8, 16], F32); M2b = Buf()
    Mt = sb("Mt", [128, 8, 16], F32); Mtb = Buf()
    tt(P, "dve", M1[:], bre2, bc(QA), ALU.mult, [ssm_bb, smb[QA]], [M1b])
    tt(P, "dve", Mt[:], bim2, bc(QB), ALU.mult, [ssm_bb, smb[QB]], [Mtb])
    tt(P, "dve", M1[:], M1[:], Mt[:], ALU.add, [M1b, Mtb], [M1b])
    tt(P, "dve", M2[:], bre2, bc(QA2), ALU.mult, [ssm_bb, smb[QA2]], [M2b])
    tt(P, "dve", Mt[:], bim2, bc(QA), ALU.mult, [ssm_bb, smb[QA]], [Mtb])
    tt(P, "dve", M2[:], M2[:], Mt[:], ALU.add, [M2b, Mtb], [M2b])
    Bm = [sb(f"Bm{i}", [128, 8, 128], BF16) for i in range(2)]
    Bmb = [Buf() for _ in range(2)]
    for i, (M, Mb) in enumerate(((M1, M1b), (M2, M2b))):
        bk, bkb = C.bank4()
        P.op("pe", lambda e, M=M, bk=bk: e.transpose(bk[:, 0:128], M[:].rearrange("p g h -> p (g h)"), ident[:]),
             reads=[Mb, identb], writes=[bkb])
        for g in range(8):
            ts(P, "dve", Bm[i][:, g, :], bk[:, 0:128], gmask[:, g:g + 1], None, ALU.mult, None, [bkb, gmaskb], [Bmb[i]])
    Cf = [sb(f"Cf{i}", [128, 8, 16], F32) for i in range(2)]
    Cfb = [Buf() for _ in range(2)]
    ts(P, "dve", Mt[:], cim2, bot, None, ALU.mult, None, [ssm_cb, hselb], [Mtb])
    stt(P, Cf[0][:], cre2, top, Mt[:], ALU.mult, ALU.subtract, [ssm_cb, Mtb, hselb], [Cfb[0]])
    ts(P, "dve", Mt[:], cre2, bot, None, ALU.mult, None, [ssm_cb, hselb], [Mtb])
    stt(P, Cf[1][:], cim2, ntop, Mt[:], ALU.mult, ALU.subtract, [ssm_cb, Mtb, hselb], [Cfb[1]])
    Cz = [sb(f"Cz{i}", [128, 8, 128], BF16) for i in range(2)]
    Czb = [Buf() for _ in range(2)]
    for i in range(2):
        mset(P, "pool", Cz[i][:], 0.0, [Czb[i]])
        for g in range(8):
            cp(P, "dve", Cz[i][:, g, g * 16:(g + 1) * 16], Cf[i][:, g, :], [Cfb[i]], [Czb[i]])
    Dm = sb("Dm", [128, 128], BF16); Dmb = Buf()
    ts(P, "dve", Dm[:], ident[:], dcol[:, 0:1], None, ALU.mult, None, [identb, dcolb], [Dmb])
    th5 = sb("th5", [128, 8], F32); th5b = Buf()
    ts(P, "dve", th5[:], V(TH), 512.0, None, ALU.mult, None, [smb[TH]], [th5b])
    sR = sb("sR", [128, 8], F32); sRb = Buf()
    cR = sb("cR", [128, 8], F32); cRb = Buf()
    sincos(C, th5, th5b, sR, sRb, cR, cRb, sc_tmp, sc_tmpb, sc_ki, sc_kib, 8)
    ts(P, "dve", sR[:], sR[:], tmb, None, ALU.mult, None, [sRb, hselb], [sRb])
    Rm = sb("Rm", [128, 8, 128], F32); Rmb = Buf()
    Rt = sb("Rt", [128, 128], F32); Rtb = Buf()
    for g in range(8):
        ts(P, "dve", Rt[:], swapI[:], sR[:, g:g + 1], None, ALU.mult, None, [swapIb, sRb], [Rtb])
        stt(P, Rm[:, g, :], ident[:], cR[:, g:g + 1], Rt[:], ALU.mult, ALU.add, [identb, cRb, Rtb], [Rmb])
    Tc = sb("Tc", [128, 8, 512], F32); Tcb = [Buf() for _ in range(8)]
    Ts = sb("Ts", [128, 8, 512], F32); Tsb = [Buf() for _ in range(8)]
    tabt = [sb(f"tabt{i}", [128, 512], F32) for i in range(3)]
    tabtb = [Buf() for _ in range(3)]
    tabk = sb("tabk", [128, 512], I32); tabkb = Buf()
    for g in range(8):
        ts(P, "dve", tabt[0][:], iota_t[:], sm[:, TH, g:g + 1], None, ALU.mult, None, [iotab, smb[TH]], [tabtb[0]])
        sincos(C, tabt[0], tabtb[0], Ts[:, g, :], Tsb[g], Tc[:, g, :], Tcb[g], [tabt[1], tabt[2]], [tabtb[1], tabtb[2]],
               tabk, tabkb, 512)

    P.barrier()
    stS.close()
    sb = sb_main
    NR_ = 2
    w1 = [sb(f"w1_{i}", [128, 512], F32) for i in range(NR_)]; w1b = [Buf() for _ in range(NR_)]
    wsc = [sb(f"wsc_{i}", [128, 512], F32) for i in range(NR_)]; wscb = [Buf() for _ in range(NR_)]
    p1 = [sb(f"p1_{i}", [128, 512], BF16) for i in range(NR_)]; p1b = [Buf() for _ in range(NR_)]
    p2 = [sb(f"p2_{i}", [128, 512], BF16) for i in range(NR_)]; p2b = [Buf() for _ in range(NR_)]
    wlast = sb("wlast", [128, 8], F32); wlastb = [Buf() for _ in range(8)]
    winit = sb("winit", [128, 8], F32); winitb = [Buf() for _ in range(8)]
    gy = [sb(f"gy{i}", [128, 512], F32) for i in range(2)]; gyb = [Buf() for _ in range(2)]
    obs = [sb(f"obs{i}", [128, 512], BF16) for i in range(2)]; obsb = [Buf() for _ in range(2)]
    oba = [sb(f"oba{i}", [128, 512], BF16) for i in range(2)]; obab = [Buf() for _ in range(2)]
    pex = [sb(f"pexa{i}", [128, 512], BF16) for i in range(3)]; pexb = [Buf() for _ in range(3)]
    accD = [sb(f"accD{i}", [128, 512], F32) for i in range(2)]; accDb = [Buf() for _ in range(2)]
    accP = [sb(f"accP{i}", [128, 512], F32) for i in range(2)]; accPb = [Buf() for _ in range(2)]
    nb = [sb(f"nb{i}", [128, NBLK], F32) for i in range(2)]; nbb = [Buf() for _ in range(2)]
    cnt = {"r": 0, "pex": 0}
    amA = dr["amA"]
    out_ops = []
    out_bufs = cfg.setdefault("a_out_bufs", [])
    Yb, Ybb = C.banks[4], C.bbuf[4]
    Ib, Ibb = C.banks[5], C.bbuf[5]
    scale = 1.0 / math.sqrt(128.0)

    SB3 = [(C.banks[i], C.bbuf[i]) for i in range(3)]
    B1, B1b = C.banks[3], C.bbuf[3]
    B2, B2b = C.banks[7], C.bbuf[7]
    Ob, Obb = C.banks[6], C.bbuf[6]
    LA = cfg.get('LA', 2)

    def ssm_gen(c):
        t0 = c * 512
        ub = pcb(uTb, t0, t0 + 512)

        def stage1(g, r):
            mm(P, B1[:, :], Bm[0][:, g, :], uT[:, t0:t0 + 512], True, True, [Bmb[0]] + ub, [B1b])
            mm(P, B2[:, :], Bm[1][:, g, :], uT[:, t0:t0 + 512], True, True, [Bmb[1]] + ub, [B2b])
            tt(P, "dve", w1[r][:], B1[:, :], Tc[:, g, :], ALU.mult, [B1b, Tcb[g]], [w1b[r]])
            tt(P, "dve", wsc[r][:], B2[:, :], Ts[:, g, :], ALU.mult, [B2b, Tsb[g]], [wscb[r]])
            tt(P, "pool", w1[r][:], w1[r][:], wsc[r][:], ALU.add, [w1b[r], wscb[r]], [w1b[r]])
            if c == 0:
                init = 0.0
                ird = []
            else:
                mm(P, Ib[:, g:g + 1], Rm[:, g, :], wlast[:, g:g + 1], True, True, [Rmb, wlastb[g]], [Ibb])
                cp(P, "dve", winit[:, g:g + 1], Ib[:, g:g + 1], [Ibb], [winitb[g]])
                init = winit[:, g:g + 1]
                ird = [winitb[g]]
            P.op("dve", lambda e: e.tensor_tensor_scan(
                out=wsc[r][:], data0=sm[:, RHO, g:g + 1].broadcast_to([128, 512]), data1=w1[r][:], initial=init,
                op0=ALU.mult, op1=ALU.add), reads=[smb[RHO], w1b[r]] + ird, writes=[wscb[r]])
            cp(P, "dve", wlast[:, g:g + 1], wsc[r][:, 511:512], [wscb[r]], [wlastb[g]])

        def stage2(g, r):
            tt(P, "pool", p1[r][:], wsc[r][:], Tc[:, g, :], ALU.mult, [wscb[r], Tcb[g]], [p1b[r]])
            tt(P, "pool", p2[r][:], wsc[r][:], Ts[:, g, :], ALU.mult, [wscb[r], Tsb[g]], [p2b[r]])
            mm(P, Yb[:, :], Cz[0][:, g, :], p1[r][:], g == 0, False, [Czb[0], p1b[r]], [Ybb])
            mm(P, Yb[:, :], Cz[1][:, g, :], p2[r][:], False, False, [Czb[1], p2b[r]], [Ybb])

        prev = None
        for g in range(8):
            r = cnt["r"] % NR_
            cnt["r"] += 1
            stage1(g, r)
            yield
            if not cfg.get('PIPE_SSM', True):
                stage2(g, r)
                yield
                continue
            if prev is not None:
                stage2(*prev)
                yield
            prev = (g, r)
        if prev is not None:
            stage2(*prev)
        mm(P, Yb[:, :], Dm[:], uT[:, t0:t0 + 512], False, True, [Dmb] + ub, [Ybb])
        yield
        act(P, gy[0][:], Yb[:, :], AF.Identity, [Ybb], [gyb[0]])
        act(P, gy[1][:], Yb[:, :], AF.Square, [Ybb], [gyb[1]])
        ts(P, "dve", gy[1][:], gy[1][:], 0.044715, 1.0, ALU.mult, ALU.add, [gyb[1]], [gyb[1]])
        tt(P, "dve", gy[1][:], gy[1][:], gy[0][:], ALU.mult, [gyb[1], gyb[0]], [gyb[1]])
        act(P, gy[1][:], gy[1][:], AF.Sigmoid, [gyb[1]], [gyb[1]], scale=2.0 * GELU_C)
        oi = c % 2
        tt(P, "dve", obs[oi][:], gy[0][:], gy[1][:], ALU.mult, [gyb[0], gyb[1]], [obsb[oi]])
        ob_ = Buf()
        out_bufs.append(ob_)
        out_ops.append(P.dma("sp", amA[1][:, t0:t0 + 512], obs[oi][:], reads=[obsb[oi]], writes=[ob_], key=f"oS{oi}"))
        yield

    def attn_gen(c):
        t0 = c * 512
        nkb = 4 * c + 4
        ni = c % 2
        ts(P, "dve", nb[ni][:, 0:nkb], Fsb[:, 0:nkb], -1.0, Fend[:, c:c + 1], ALU.mult, ALU.add, [Fsbb, Fendb], [nbb[ni]])
        mset(P, "pool", accD[ni][:], 0.0, [accDb[ni]])
        mset(P, "pool", accP[ni][:], 0.0, [accPb[ni]])
        qb_ = pcb(qTb, t0, t0 + 512)
        yield

        def emit_s(kb):
            j = kb - 4 * c
            c0 = 128 * j if j > 0 else 0
            bS, bSb = SB3[cnt["sb"] % 3]
            cnt["sb"] += 1
            kbufs = pcb(kTb, kb * 128, kb * 128 + 128)
            mm(P, bS[:, c0:512], kT[:, kb * 128:(kb + 1) * 128], qT[:, t0 + c0:t0 + 512], True, True, kbufs + qb_, [bSb])
            pi = cnt["pex"] % 3
            cnt["pex"] += 1
            act(P, pex[pi][:, c0:512], bS[:, c0:512], AF.Exp, [bSb, nbb[ni]], [pexb[pi]], bias=nb[ni][:, kb:kb + 1], scale=scale)
            if j >= 0:
                tt(P, "pool", pex[pi][:, c0:c0 + 128], pex[pi][:, c0:c0 + 128], mdiag[:], ALU.mult, [pexb[pi], mdiagb], [pexb[pi]])
            return (kb, c0, pi)

        def emit_pv(kb, c0, pi):
            mm(P, Ob[:, c0:512], vtok[:, kb, :], pex[pi][:, c0:512], kb == 0, kb == nkb - 1,
               pcb(vtb, kb * 128, kb * 128 + 128) + [pexb[pi]], [Obb])
            if kb % 2 == 0:
                tt(P, "dve", accD[ni][:, c0:512], accD[ni][:, c0:512], pex[pi][:, c0:512], ALU.add, [accDb[ni], pexb[pi]], [accDb[ni]])
            else:
                tt(P, "pool", accP[ni][:, c0:512], accP[ni][:, c0:512], pex[pi][:, c0:512], ALU.add, [accPb[ni], pexb[pi]], [accPb[ni]])

        pend = []
        for kb in range(nkb):
            pend.append(emit_s(kb))
            if len(pend) > LA:
                emit_pv(*pend.pop(0))
            yield
        while pend:
            emit_pv(*pend.pop(0))
        yield
        tt(P, "dve", accD[ni][:], accD[ni][:], accP[ni][:], ALU.add, [accDb[ni], accPb[ni]], [accDb[ni]])
        bZ, bZb = SB3[cnt["sb"] % 3]
        cnt["sb"] += 1
        mm(P, bZ[:, :], ones32[:], accD[ni][:], True, True, [ones32b, accDb[ni]], [bZb])
        P.op("dve", lambda e: e.reciprocal(out=accP[ni][:], in_=bZ[:, :]), reads=[bZb], writes=[accPb[ni]])
        tt(P, "dve", oba[ni][:], Ob[:, :], accP[ni][:], ALU.mult, [Obb, accPb[ni]], [obab[ni]])
        ob_ = Buf()
        out_bufs.append(ob_)
        out_ops.append(P.dma("sp", amA[0][:, t0:t0 + 512], oba[ni][:], reads=[obab[ni]], writes=[ob_], key=f"oA{ni}"))
        yield

    cnt["sb"] = 0
    for _ in ssm_gen(0):
        pass
    for c in range(NQ):
        ga = attn_gen(c)
        gs = ssm_gen(c + 1) if c + 1 < NQ else None
        n_a = 4 * c + 4 + 3
        n_s = 18
        every = max(1, n_a // n_s)
        step = 0
        if not cfg.get('INTERLEAVE', False):
            for _ in ga:
                pass
        for _ in ga:
            step += 1
            if gs is not None and step % every == 0:
                if next(gs, "done") == "done":
                    gs = None
        if gs is not None:
            for _ in gs:
                pass
    return out_ops


def build_a(cfg):
    nc = bass.Bass("TRN2", target_bir_lowering=False)
    dr = {}
    for name, shp, dt in A_INPUTS:
        dr[name] = nc.dram_tensor(name, shp(cfg), dt, kind="ExternalInput").ap()
    dr["amA"] = nc.dram_tensor("amA", [2, 128, cfg["L"]], BF16, kind="ExternalOutput").ap()
    C = Ctx(nc)
    with contextlib.ExitStack() as st:
        C.alloc_banks(st)
        outs = phase_a(C, st, dr, cfg)
        C.P.emit(final_wait_ops=outs)
    return nc


def prep_a(inp, cfg):
    L = cfg["L"]
    x = np.asarray(inp["x"], np.float32)[0]
    xT = np.ascontiguousarray(x.T)
    w_in = np.asarray(inp["w_in_ab"], np.float32)[0]
    w_ada = np.asarray(inp["w_ada"], np.float32)
    b_ada = np.asarray(inp["b_ada"], np.float32)[:, None, :]
    lre = np.asarray(inp["ssm_lambda_re"], np.float32)[0]
    lim = np.asarray(inp["ssm_lambda_im"], np.float32)[0]
    ldt = np.asarray(inp["ssm_log_dt"], np.float32)[0]
    bre = np.asarray(inp["ssm_b_re"], np.float32)[0]
    bim = np.asarray(inp["ssm_b_im"], np.float32)[0]
    cre = np.asarray(inp["ssm_c_re"], np.float32)[0]
    cim = np.asarray(inp["ssm_c_im"], np.float32)[0]
    dsk = np.asarray(inp["ssm_d"], np.float32)[0]
    bfg = np.asarray(inp["b_forget"], np.float32)[0]
    hsel = np.zeros((128, 5), np.float32)
    hsel[:64, 0] = 1; hsel[64:, 1] = 1; hsel[:64, 2] = -1; hsel[64:, 3] = -1; hsel[:64, 4] = 1; hsel[64:, 4] = -1
    gmask = np.zeros((128, 8), np.float32)
    for p in range(128):
        gmask[p, p // 16] = 1
    iota_t = np.ascontiguousarray(np.broadcast_to(np.arange(512, dtype=np.float32)[None], (128, 512)))
    swapI = np.zeros((128, 128), np.float32)
    for p in range(128):
        swapI[p, (p + 64) % 128] = 1
    ident = np.eye(128, dtype=np.float32)
    shared = dict(xT=xT, cT=fm(inp["c"][0]), w_ada=w_ada, b_ada=b_ada, hsel=hsel, gmask=gmask, iota_t=iota_t,
                  swapI=swapI, ident=ident)
    maps = []
    dup = lambda a: np.concatenate([a, a], axis=0)
    for i in range(NCORE):
        gs = slice(8 * i, 8 * i + 8)
        W = 1024
        cols = np.concatenate([w_in[:, i * 128:(i + 1) * 128], w_in[:, W + i * 128:W + (i + 1) * 128],
                               w_in[:, 3 * W + 8 + i * 128:3 * W + 8 + (i + 1) * 128],
                               w_in[:, 2 * W + i * 128:2 * W + (i + 1) * 128], w_in[:, 3 * W + i:3 * W + i + 1]], axis=1)
        ssm_s = np.stack([dup(lre[gs].T), dup(lim[gs].T), np.broadcast_to(ldt[gs][None, :], (128, 8))], axis=1)
        ssm_b = np.stack([dup(bre[gs].transpose(1, 0, 2)), dup(bim[gs].transpose(1, 0, 2))], axis=1)
        ssm_c = np.stack([dup(cre[gs].transpose(2, 0, 1)), dup(cim[gs].transpose(2, 0, 1))], axis=1)
        m = dict(shared)
        m.update(w_inA=np.ascontiguousarray(cols), bf=np.full((128, 1), bfg[i], np.float32),
                 ssm_s=np.ascontiguousarray(ssm_s.astype(np.float32)), ssm_b=np.ascontiguousarray(ssm_b),
                 ssm_c=np.ascontiguousarray(ssm_c), dcol=np.ascontiguousarray(dsk[gs].reshape(128, 1)))
        maps.append(m)
    return maps


def gather_a(res, cfg):
    L = cfg["L"]
    am = np.zeros((16, 128, L), ml_dtypes.bfloat16)
    for i, r in enumerate(res.results):
        a = np.asarray(r["amA"])
        am[i] = a[0]
        am[8 + i] = a[1]
    return am


_NC_CACHE = {}


def kernel_unfused(**inputs):
    L = np.asarray(inputs["x"]).shape[1]
    cfg = make_cfg(L)
    inp = {k: np.asarray(v) for k, v in inputs.items()}
    nca = build_a(cfg)
    ncb = build_b(make_cfg(L))
    resA = run_bass_kernel_spmd(nca, prep_a(inp, cfg), core_ids=list(range(NCORE)))
    amT = gather_a(resA, cfg)
    resB = run_bass_kernel_spmd(ncb, prep_b(inp, amT, cfg), core_ids=list(range(NCORE)))
    return gather_b(resB, cfg)


def build_fused(cfg):
    nc = bass.Bass("TRN2", target_bir_lowering=False)
    L, NTOK, NT = cfg["L"], cfg["NTOK"], cfg["NT"]
    dr = {}
    seen = set()
    for name, shp, dt in A_INPUTS:
        dr[name] = nc.dram_tensor(name, shp(cfg), dt, kind="ExternalInput").ap()
        seen.add(name)
    for name, shp, dt in B_INPUTS:
        if name == "amT" or name in ("cT", "w_ada", "b_ada"):
            continue
        nm = "xTb" if name == "xT" else name
        dr[nm] = nc.dram_tensor(nm, shp(cfg), dt, kind="ExternalInput").ap()
    dr["yT"] = nc.dram_tensor("yT", [2048, NTOK], F32, kind="ExternalOutput").ap()
    WC = HALO + L
    cins = [nc.dram_tensor(f"cc_in{h}", [128, WC], BF16) for h in range(2)]
    couts = [nc.dram_tensor(f"cc_out{h}", [NCORE * 128, WC], BF16) for h in range(2)]
    dr["amA"] = [cins[h].ap()[:, HALO:WC] for h in range(2)]
    C = Ctx(nc)
    P = C.P
    cfg["a_out_bufs"] = []
    with contextlib.ExitStack() as st:
        C.alloc_banks(st)
        with contextlib.ExitStack() as stA:
            cfg["zpad_ap"] = [cins[h].ap()[:, 0:HALO] for h in range(2)]
            phase_a(C, stA, dr, cfg)
            P.barrier()
        bcout = [Buf(), Buf()]
        for h in range(2):
            def cc(e, h=h):
                return e.collective_compute("AllGather", ALU.bypass, replica_groups=[list(range(NCORE))],
                                            ins=[cins[h].ap().opt()], outs=[couts[h].ap().opt()])
            P.op("pool", cc, reads=cfg["a_out_bufs"], writes=[bcout[h]], is_dma=True, semkey=f"cc{h}", inc=1)
        cout_v = [couts[h].ap().rearrange("(r p) t -> p r t", r=NCORE, p=128) for h in range(2)]

        def am_loader(P_, gb, t0, tg, wr):
            for h in range(2):
                def f(e, h=h):
                    pid = e.partition_id()
                    return e.dma_start(out=gb[:, 8 * h:8 * h + 8, 0:tg], in_=cout_v[h][:, :, bass.ds(pid * NTOK + t0, tg)])
                P_.op("sp", f, reads=[bcout[h]], writes=wr(h), is_dma=True, semkey="ain")
        cfgb = dict(cfg)
        cfgb["am_loader"] = am_loader
        cfgb["xb_name"] = "xTb"
        outs = phase_b(C, st, dr, cfgb)
        P.emit(final_wait_ops=outs)
    return nc


def kernel(**inputs):
    L = np.asarray(inputs["x"]).shape[1]
    cfg = make_cfg(L)
    inp = {k: np.asarray(v) for k, v in inputs.items()}
    nc = build_fused(cfg)
    ma = prep_a(inp, cfg)
    mb = prep_b(inp, None, cfg)
    maps = []
    for a, b in zip(ma, mb):
        m = dict(a)
        for k, v in b.items():
            if k in ("amT", "cT", "w_ada", "b_ada"):
                continue
            m["xTb" if k == "xT" else k] = v
        maps.append(m)
    res = run_bass_kernel_spmd(nc, maps, core_ids=list(range(NCORE)))
    return gather_b(res, cfg)
```

```python
import math
import contextlib
import numpy as np
import ml_dtypes
import concourse.bass as bass
import concourse.mybir as mybir
from concourse.bass_utils import run_bass_kernel_spmd

F32 = mybir.dt.float32
BF16 = mybir.dt.bfloat16
I32 = mybir.dt.int32
AF = mybir.ActivationFunctionType
ALU = mybir.AluOpType

D = 2048
KC = 16
DFF = 5632
JC = 44
NCORE = 8
ALPHA = math.sqrt(2.0)
LN_EPS = 1e-5
HALO = 128
TWO_PI = 2.0 * math.pi
CW1 = 6.28125
CW2 = TWO_PI - CW1
GELU_C = math.sqrt(2.0 / math.pi)


class Buf:
    __slots__ = ("name", "w", "r")

    def __init__(self, name=""):
        self.name = name
        self.w = None
        self.r = []


class Op:
    __slots__ = ("eng", "fn", "deps", "signal", "cnt", "semkey", "is_dma", "inc")

    def __init__(self, eng, fn, is_dma=False, semkey=None, inc=16):
        self.inc = inc
        self.eng = eng
        self.fn = fn
        self.deps = []
        self.signal = False
        self.cnt = None
        self.semkey = semkey
        self.is_dma = is_dma


class Prog:
    ENGS = ("pe", "act", "dve", "pool", "sp")

    def __init__(self, nc, same_engine_sync=("act", "dve", "pool")):
        self.nc = nc
        self.ops = {e: [] for e in self.ENGS}
        self.same = set(same_engine_sync)
        self.order = []
        self.pending = {e: [] for e in self.ENGS}
        self.last_dma = {}

    def op(self, eng, fn, reads=(), writes=(), is_dma=False, semkey=None, inc=16):
        o = Op(eng, fn, is_dma, semkey, inc)
        deps = list(self.pending[eng])
        self.pending[eng] = []
        for b in reads:
            if b.w is not None:
                deps.append(b.w)
        for b in writes:
            if b.w is not None:
                deps.append(b.w)
            deps.extend(b.r)
        for b in reads:
            if not is_dma:
                b.r = [x for x in b.r if x.is_dma or x.eng != eng]
            b.r.append(o)
        for b in writes:
            b.w = o
            b.r = []
        seen = set()
        for d in deps:
            if id(d) in seen or d is o:
                continue
            seen.add(id(d))
            if (not d.is_dma) and d.eng == eng and eng not in self.same:
                continue
            o.deps.append(d)
            d.signal = True
        self.ops[eng].append(o)
        self.order.append(o)
        if is_dma:
            self.last_dma[semkey] = o
        return o

    def dma(self, eng, out, in_, reads=(), writes=(), key=None, **kw):
        return self.op(eng, lambda e: e.dma_start(out=out, in_=in_, **kw), reads, writes,
                       is_dma=True, semkey=key)

    def barrier(self):
        lasts = []
        for e in self.ENGS:
            for o in reversed(self.ops[e]):
                if not o.is_dma:
                    lasts.append(o)
                    break
        lasts.extend(self.last_dma.values())
        for e in self.ENGS:
            self.pending[e] = list(lasts)

    def emit(self, final_wait_ops=()):
        nc = self.nc
        keys = []
        for o in self.order:
            if o.is_dma and o.semkey not in keys:
                keys.append(o.semkey)
        with contextlib.ExitStack() as st:
            sems = {}
            for e in self.ENGS:
                sems[("eng", e)] = st.enter_context(nc.semaphore("s_" + e))
            for k in keys:
                sems[("dma", k)] = st.enter_context(nc.semaphore("d_" + str(k)))
            cnts = {}
            for o in self.order:
                if o.is_dma:
                    k = ("dma", o.semkey)
                    cnts[k] = cnts.get(k, 0) + o.inc
                    o.cnt = cnts[k]
                elif o.signal:
                    k = ("eng", o.eng)
                    cnts[k] = cnts.get(k, 0) + 1
                    o.cnt = cnts[k]
            block = st.enter_context(nc.Block())
            engmap = {"pe": block.tensor, "act": block.scalar, "dve": block.vector,
                      "pool": block.gpsimd, "sp": block.sync}

            def make(ename):
                def body(eng):
                    known = {}
                    for o in self.ops[ename]:
                        need = {}
                        for d in o.deps:
                            k = ("dma", d.semkey) if d.is_dma else ("eng", d.eng)
                            if d.cnt > need.get(k, 0):
                                need[k] = d.cnt
                        for k, v in need.items():
                            if known.get(k, 0) >= v:
                                continue
                            eng.wait_ge(sems[k], v)
                            known[k] = v
                        ins = o.fn(eng)
                        if o.is_dma:
                            ins.then_inc(sems[("dma", o.semkey)], o.inc)
                        elif o.signal:
                            ins.then_inc(sems[("eng", ename)], 1)
                    if ename == "sp":
                        for o in final_wait_ops:
                            k = ("dma", o.semkey) if o.is_dma else ("eng", o.eng)
                            eng.wait_ge(sems[k], o.cnt)
                return body

            for e in self.ENGS:
                engmap[e](make(e))


class Ctx:
    def __init__(self, nc):
        self.nc = nc
        self.P = Prog(nc)
        self.uid = 0

    def sb(self, st, name, shape, dt):
        self.uid += 1
        return st.enter_context(self.nc.sbuf_tensor(f"{name}_{self.uid}", shape, dt))

    def alloc_banks(self, st):
        self.banks = []
        self.bbuf = []
        for i in range(8):
            self.banks.append(st.enter_context(self.nc.psum_tensor(f"bank{i}", [128, 512], F32)))
            self.bbuf.append(Buf(f"bank{i}"))
        self.rr = 0

    def bank4(self):
        i = self.rr % 4
        self.rr += 1
        return self.banks[i], self.bbuf[i]


class WStream:
    def __init__(self, C, st, name, ns, nk, ncols):
        self.C = C
        self.ns = ns
        self.name = name
        self.slots = [C.sb(st, f"{name}{i}", [128, nk, ncols], BF16) for i in range(ns)]
        self.bufs = [Buf(f"{name}{i}") for i in range(ns)]
        self.i = 0

    def load(self, src, nk, ncols=None, col0=0, fresh=True):
        if fresh:
            self.i += 1
        s = (self.i - 1) % self.ns
        nco = src.shape[1]
        self.C.P.dma("pool", self.slots[s][:, 0:nk, col0:col0 + nco],
                     src.rearrange("(k p) n -> p k n", p=128),
                     writes=[self.bufs[s]], key=f"{self.name}{s}")
        return self.slots[s], self.bufs[s]


def sincos(C, arg, argb, sin_t, sinb, cos_t, cosb, tmp, tmpb, ki, kib, w):
    P = C.P
    u, kf = tmp
    ub, kfb = tmpb
    P.op("dve", lambda e: e.tensor_scalar(out=u[:, :w], in0=arg[:, :w], scalar1=1.0 / TWO_PI, scalar2=None,
                                          op0=ALU.mult), reads=[argb], writes=[ub])
    P.op("dve", lambda e: e.tensor_copy(out=ki[:, :w], in_=u[:, :w]), reads=[ub], writes=[kib])
    P.op("dve", lambda e: e.tensor_copy(out=kf[:, :w], in_=ki[:, :w]), reads=[kib], writes=[kfb])
    P.op("dve", lambda e: e.scalar_tensor_tensor(out=u[:, :w], in0=kf[:, :w], scalar=-CW1, in1=arg[:, :w],
                                                 op0=ALU.mult, op1=ALU.add), reads=[kfb, argb], writes=[ub])
    P.op("dve", lambda e: e.scalar_tensor_tensor(out=u[:, :w], in0=kf[:, :w], scalar=-CW2, in1=u[:, :w],
                                                 op0=ALU.mult, op1=ALU.add), reads=[kfb, ub], writes=[ub])
    for (dst, dstb, shift) in ((sin_t, sinb, 0.0), (cos_t, cosb, math.pi / 2)):
        src = u
        if shift != 0.0:
            P.op("dve", lambda e, dst=dst: e.tensor_scalar(out=dst[:, :w], in0=u[:, :w], scalar1=shift, scalar2=None,
                                                           op0=ALU.add), reads=[ub], writes=[dstb])
            src = dst
            srcb = dstb
        else:
            srcb = ub
        P.op("dve", lambda e, src=src: e.tensor_single_scalar(out=kf[:, :w], in_=src[:, :w], scalar=math.pi, op=ALU.is_gt),
             reads=[srcb], writes=[kfb])
        P.op("dve", lambda e, src=src, dst=dst: e.scalar_tensor_tensor(out=dst[:, :w], in0=kf[:, :w], scalar=-TWO_PI,
                                                                       in1=src[:, :w], op0=ALU.mult, op1=ALU.add),
             reads=[kfb, srcb], writes=[dstb])
        P.op("dve", lambda e, dst=dst: e.tensor_single_scalar(out=kf[:, :w], in_=dst[:, :w], scalar=-math.pi, op=ALU.is_lt),
             reads=[dstb], writes=[kfb])
        P.op("dve", lambda e, dst=dst: e.scalar_tensor_tensor(out=dst[:, :w], in0=kf[:, :w], scalar=TWO_PI,
                                                              in1=dst[:, :w], op0=ALU.mult, op1=ALU.add),
             reads=[kfb, dstb], writes=[dstb])
        P.op("dve", lambda e, dst=dst: e.tensor_scalar(out=dst[:, :w], in0=dst[:, :w], scalar1=-math.pi, scalar2=math.pi,
                                                       op0=ALU.max, op1=ALU.min), reads=[dstb], writes=[dstb])
        P.op("act", lambda e, dst=dst: e.activation(out=dst[:, :w], in_=dst[:, :w], func=AF.Sin),
             reads=[dstb], writes=[dstb])


def gen_mod(C, st, w_ada_l, b_ada_l, scb, scb_buf, modT, modb, which, one11, oneb, tag):
    P = C.P
    ws = WStream(C, st, "wada" + tag, 2, 16, 512)
    brow = [C.sb(st, f"brow{i}", [1, 512], F32) for i in range(2)]
    browb = [Buf() for _ in range(2)]
    mrow = [C.sb(st, f"mrow{i}", [1, 512], F32) for i in range(2)]
    mrowb = [Buf() for _ in range(2)]
    it = 0
    for w in which:
        for half in range(4):
            col0 = w * 2048 + half * 512
            slot, sbuf = ws.load(w_ada_l[:, col0:col0 + 512], 16)
            i = it % 2
            it += 1
            P.dma("sp", brow[i][:], b_ada_l[:, col0:col0 + 512], writes=[browb[i]], key=f"brow{tag}{i}")
            bk, bkb = C.bank4()
            for k in range(16):
                P.op("pe", lambda e, k=k, slot=slot, bk=bk: e.matmul(bk[0:1, :], scb[:, k:k + 1], slot[:, k, :],
                                                                     start=(k == 0), stop=(k == 15)),
                     reads=[scb_buf, sbuf], writes=[bkb])
            P.op("dve", lambda e, i=i, bk=bk: e.tensor_tensor(out=mrow[i][:], in0=bk[0:1, :], in1=brow[i][:], op=ALU.add),
                 reads=[bkb, browb[i]], writes=[mrowb[i]])
            bk2, bk2b = C.bank4()
            for c in range(4):
                P.op("pe", lambda e, c=c, i=i, bk2=bk2: e.matmul(bk2[:, c:c + 1], mrow[i][0:1, c * 128:(c + 1) * 128],
                                                                 one11[0:1, 0:1], start=True, stop=True),
                     reads=[mrowb[i], oneb], writes=[bk2b])
            q0 = w * 16 + half * 4
            P.op("dve", lambda e, q0=q0, bk2=bk2: e.tensor_copy(out=modT[:, q0:q0 + 4], in_=bk2[:, 0:4]),
                 reads=[bk2b], writes=[modb])


def mm(P, out, lhsT, rhs, start, stop, reads, writes):
    return P.op("pe", lambda e: e.matmul(out, lhsT, rhs, start=start, stop=stop), reads=reads, writes=writes)


def act(P, out, in_, func, reads, writes, bias=None, scale=None):
    kw = {}
    if bias is not None:
        kw["bias"] = bias
    if scale is not None:
        kw["scale"] = scale
    return P.op("act", lambda e: e.activation(out=out, in_=in_, func=func, **kw), reads=reads, writes=writes)


def tt(P, eng, out, in0, in1, op, reads, writes):
    return P.op(eng, lambda e: e.tensor_tensor(out=out, in0=in0, in1=in1, op=op), reads=reads, writes=writes)


def ts(P, eng, out, in0, s1, s2, op0, op1, reads, writes):
    if s2 is None:
        return P.op(eng, lambda e: e.tensor_scalar(out=out, in0=in0, scalar1=s1, scalar2=None, op0=op0),
                    reads=reads, writes=writes)
    return P.op(eng, lambda e: e.tensor_scalar(out=out, in0=in0, scalar1=s1, scalar2=s2, op0=op0, op1=op1),
                reads=reads, writes=writes)


def stt(P, out, in0, scalar, in1, op0, op1, reads, writes):
    return P.op("dve", lambda e: e.scalar_tensor_tensor(out=out, in0=in0, scalar=scalar, in1=in1, op0=op0, op1=op1),
                reads=reads, writes=writes)


def cp(P, eng, out, in_, reads, writes):
    return P.op(eng, lambda e: e.tensor_copy(out=out, in_=in_), reads=reads, writes=writes)


def mset(P, eng, ap, val, writes):
    return P.op(eng, lambda e: e.memset(ap, val), writes=writes)


def asel(P, out, in_, pattern, cmp, fill, base, cm, reads, writes):
    return P.op("pool", lambda e: e.affine_select(out=out, in_=in_, pattern=pattern, compare_op=cmp, fill=fill,
                                                  base=base, channel_multiplier=cm), reads=reads, writes=writes)


def phase_b(C, st, dr, cfg, mods=None):
    P = C.P
    nc = C.nc
    TG = cfg["TG"]
    groups = cfg["groups"]
    NBMAX = TG // 128
    sb = lambda name, shape, dt: C.sb(st, name, shape, dt)

    ones32 = sb("ones32", [128, 128], F32); ones32b = Buf()
    mset(P, "dve", ones32[:], 1.0, [ones32b])
    ones64 = sb("ones64", [128, 64], BF16); ones64b = Buf()
    mset(P, "dve", ones64[:], 1.0, [ones64b])
    one11 = sb("one11", [1, 1], F32); one11b = Buf()
    mset(P, "dve", one11[:], 1.0, [one11b])
    mcur = sb("mcur", [128, 4, 128], BF16); mcurb = Buf()
    mprev = sb("mprev", [128, 4, 128], BF16); mprevb = Buf()
    mset(P, "pool", mcur[:], 1.0, [mcurb])
    mset(P, "pool", mprev[:], 1.0, [mprevb])
    asel(P, mcur[:], mcur[:], [[0, 4], [1, 128]], ALU.is_ge, 0.0, 0, -1, [mcurb], [mcurb])
    asel(P, mprev[:], mprev[:], [[0, 4], [-1, 128]], ALU.is_gt, 0.0, 0, 1, [mprevb], [mprevb])

    cT = sb("cT", [128, 16], F32); cTb = Buf()
    P.dma("sp", cT[:], dr["cT"], writes=[cTb], key="c0")
    scb = sb("scb", [128, 16], BF16); scbb = Buf()
    act(P, scb[:], cT[:], AF.Silu, [cTb], [scbb])
    kbias = sb("kbias", [128, 2], F32); kbiasb = Buf()
    P.dma("sp", kbias[:], dr["kbias"], writes=[kbiasb], key="c1")
    invf = sb("invf", [128, 1], F32); invfb = Buf()
    P.dma("sp", invf[:], dr["invf"], writes=[invfb], key="c2")
    perm = sb("perm", [128, 128], F32); permb = Buf()
    P.dma("sp", perm[:], dr["perm"], writes=[permb], key="c3")
    es = sb("es", [128, 32], F32); esb = Buf()
    P.dma("sp", es[:], dr["sinkb"], writes=[esb], key="c4")
    act(P, es[:], es[:], AF.Exp, [esb], [esb])
    bglu = sb("bglu", [128, 8], F32); bglub = Buf()
    P.dma("sp", bglu[:], dr["b_gluT"], writes=[bglub], key="c5")
    lnp = sb("lnp", [128, 8, 16], F32); lnpb = Buf()
    P.dma("sp", lnp[:], dr["lnp"], writes=[lnpb], key="c6")
    ts(P, "dve", lnp[:, 0:6, :], lnp[:, 0:6, :], ALPHA, None, ALU.mult, None, [lnpb], [lnpb])

    modT = [sb(f"modT{l}", [128, 96], F32) for l in range(2)]
    modb = [Buf() for _ in range(2)]
    with contextlib.ExitStack() as st2:
        gen_mod(C, st2, dr["w_ada"][0], dr["b_ada"][0], scb, scbb, modT[0], modb[0], [2, 3, 4, 5], one11, one11b, "a")
        gen_mod(C, st2, dr["w_ada"][1], dr["b_ada"][1], scb, scbb, modT[1], modb[1], [0, 1, 2, 3, 4, 5], one11, one11b, "b")
        P.barrier()
    for l in range(2):
        for w in (1, 4):
            ts(P, "dve", modT[l][:, w * 16:(w + 1) * 16], modT[l][:, w * 16:(w + 1) * 16], 1.0, 1.0 / ALPHA,
               ALU.add, ALU.mult, [modb[l]], [modb[l]])

    def mcol(l, w, m):
        return modT[l][:, w * 16 + m:w * 16 + m + 1]

    xa = sb("xa", [128, 16, TG], F32)
    hb = sb("hb", [128, 16, TG], BF16)
    gb = sb("gb", [128, 16, TG], BF16)
    NS2 = (TG + 511) // 512
    xab = [[Buf(f"xa{m}_{s}") for s in range(NS2)] for m in range(16)]
    hbb = [[Buf(f"hb{m}_{s}") for s in range(NS2)] for m in range(16)]
    gbb = [[Buf(f"gb{m}_{s}") for s in range(NS2)] for m in range(16)]
    kT2 = sb("kT2", [128, 4, TG + 128], BF16)
    kT2b = [Buf(f"k{b}") for b in range(NBMAX + 1)]
    vsb = sb("vsb", [128, NBMAX + 1, 256], BF16)
    vsbb = [Buf(f"v{b}") for b in range(NBMAX + 1)]
    cos_t = sb("cos_t", [128, TG], F32); cosb = Buf()
    sin_t = sb("sin_t", [128, TG], F32); sinb = Buf()
    posi = sb("posi", [128, 512], I32); posib = Buf()
    tmpf = [sb(f"tmpf{i}", [128, 512], F32) for i in range(2)]
    tmpfb = [Buf() for _ in range(2)]
    sq = [sb(f"sq{i}", [128, 512], F32) for i in range(2)]
    sqb = [Buf() for _ in range(2)]
    rot = [sb(f"rot{i}", [128, 512], F32) for i in range(2)]
    rotb = [Buf() for _ in range(2)]
    stm = sb("stm", [128, 512], F32); stmb = Buf()
    str_ = sb("str", [128, 512], F32); strb = Buf()
    stn = sb("stn", [128, 512], F32); stnb = Buf()
    pex = [sb(f"pex{i}", [128, 4, 128], BF16) for i in range(4)]
    pexb = [Buf() for _ in range(4)]
    den = [sb(f"den{i}", [128, 4, 128], F32) for i in range(2)]
    denb = [Buf() for _ in range(2)]
    ws = WStream(C, st, "ws", 4, 16, 128)
    cnt = {"tmpf": 0, "sq": 0, "rot": 0, "pex": 0, "den": 0}

    def nxt(name, n):
        i = cnt[name] % n
        cnt[name] += 1
        return i

    xT_v = dr[cfg.get("xb_name", "xT")].rearrange("(k p) t -> p k t", p=128)
    amT_v = dr["amT"].rearrange("c p t -> p c t") if "amT" in dr else None
    yT_v = dr["yT"].rearrange("(k p) t -> p k t", p=128)
    out_ops = []

    pend_stats = []

    def flush_stats():
        while pend_stats:
            (m, si, off, w, first, last, i) = pend_stats.pop(0)
            s1, s1b = C.banks[4 + 2 * si], C.bbuf[4 + 2 * si]
            s2, s2b = C.banks[5 + 2 * si], C.bbuf[5 + 2 * si]
            mm(P, s1[:, :w], ones32[:], xa[:, m, off:off + w], first, last, [ones32b, xab[m][si]], [s1b])
            mm(P, s2[:, :w], ones32[:], sq[i][:, :w], first, last, [ones32b, sqb[i]], [s2b])

    def resid(bank, bankb, m, si, off, w, gate_col, stats, first, last):
        flush_stats()
        stt(P, xa[:, m, off:off + w], bank[:, :w], gate_col, xa[:, m, off:off + w], ALU.mult, ALU.add,
            [bankb, xab[m][si]], [xab[m][si]])
        if stats:
            i = nxt("sq", 2)
            act(P, sq[i][:, :w], xa[:, m, off:off + w], AF.Square, [xab[m][si]], [sqb[i]])
            pend_stats.append((m, si, off, w, first, last, i))

    def ln_finish(si, off, w, gi_, bi_, A_l, A_w, sh_w, final):
        flush_stats()
        s1, s1b = C.banks[4 + 2 * si], C.bbuf[4 + 2 * si]
        s2, s2b = C.banks[5 + 2 * si], C.bbuf[5 + 2 * si]
        ts(P, "dve", stm[:, :w], s1[:, :w], 1.0 / D, None, ALU.mult, None, [s1b], [stmb])
        tt(P, "dve", stn[:, :w], stm[:, :w], stm[:, :w], ALU.mult, [stmb], [stnb])
        stt(P, str_[:, :w], s2[:, :w], 1.0 / D, stn[:, :w], ALU.mult, ALU.subtract, [s2b, stnb], [strb])
        ts(P, "dve", str_[:, :w], str_[:, :w], LN_EPS, None, ALU.add, None, [strb], [strb])
        act(P, str_[:, :w], str_[:, :w], AF.Sqrt, [strb], [strb])
        P.op("dve", lambda e: e.reciprocal(out=str_[:, :w], in_=str_[:, :w]), reads=[strb], writes=[strb])
        stt(P, stn[:, :w], stm[:, :w], -1.0, str_[:, :w], ALU.mult, ALU.mult, [stmb, strb], [stnb])
        for m in range(16):
            xs = xa[:, m, off:off + w]
            tt(P, "dve", xs, xs, str_[:, :w], ALU.mult, [xab[m][si], strb], [xab[m][si]])
            tt(P, "dve", xs, xs, stn[:, :w], ALU.add, [xab[m][si], stnb], [xab[m][si]])
            ts(P, "dve", xs, xs, lnp[:, gi_, m:m + 1], lnp[:, bi_, m:m + 1], ALU.mult, ALU.add,
               [xab[m][si], lnpb], [xab[m][si]])
            if not final:
                act(P, hb[:, m, off:off + w], xs, AF.Identity, [xab[m][si], modb[A_l]], [hbb[m][si]],
                    bias=mcol(A_l, sh_w, m), scale=mcol(A_l, A_w, m))

    def ffn(l, subs, gate_w, ln_g, ln_b, nxt_mod, final):
        wg_l, wu_l, wd_l = dr["wg"][l], dr["wu"][l], dr["wd"][l]
        NPASS, JP = 4, 11
        for hp in range(NPASS):
            for jj in range(JP):
                j = hp * JP + jj
                wgs, wgb = ws.load(wg_l[:, j * 128:(j + 1) * 128], 16)
                wus, wub = ws.load(wu_l[:, j * 128:(j + 1) * 128], 16)
                for si, (off, w) in enumerate(subs):
                    bg, bgb = C.bank4()
                    bu, bub = C.bank4()
                    for k in range(16):
                        mm(P, bg[:, :w], wgs[:, k, :], hb[:, k, off:off + w], k == 0, k == 15, [wgb, hbb[k][si]], [bgb])
                    for k in range(16):
                        mm(P, bu[:, :w], wus[:, k, :], hb[:, k, off:off + w], k == 0, k == 15, [wub, hbb[k][si]], [bub])
                    i = nxt("tmpf", 2)
                    act(P, tmpf[i][:, :w], bg[:, :w], AF.Silu, [bgb], [tmpfb[i]])
                    tt(P, "dve", gb[:, jj, off:off + w], tmpf[i][:, :w], bu[:, :w], ALU.mult, [tmpfb[i], bub], [gbb[jj][si]])
            for m in range(16):
                wds, wdb = ws.load(wd_l[hp * JP * 128:(hp + 1) * JP * 128, m * 128:(m + 1) * 128], JP)
                for si, (off, w) in enumerate(subs):
                    bk, bkb = C.bank4()
                    for jj in range(JP):
                        mm(P, bk[:, :w], wds[:, jj, :], gb[:, jj, off:off + w], jj == 0, jj == JP - 1, [wdb, gbb[jj][si]], [bkb])
                    resid(bk, bkb, m, si, off, w, mcol(l, gate_w, m), hp == NPASS - 1, m == 0, m == 15)
        for si, (off, w) in enumerate(subs):
            ln_finish(si, off, w, ln_g, ln_b, nxt_mod[0], nxt_mod[1], nxt_mod[2], final)

    first_main = True
    for gi, g in enumerate(groups):
        t0, tg, halo = g["t0"], g["tg"], g["halo"]
        subs = [(o, min(512, tg - o)) for o in range(0, tg, 512)]
        nblk = tg // 128
        allx = [xab[m][si] for m in range(16) for si in range(len(subs))]
        P.dma("sp", xa[:, :, 0:tg], xT_v[:, :, t0:t0 + tg], writes=allx, key="xin")
        if "am_loader" in cfg:
            cfg["am_loader"](P, gb, t0, tg, lambda h: [gbb[8 * h + m][si] for m in range(8) for si in range(len(subs))])
        else:
            P.dma("sp", gb[:, 0:16, 0:tg], amT_v[:, :, t0:t0 + tg],
                  writes=[gbb[m][si] for m in range(16) for si in range(len(subs))], key="ain")
        for m in range(16):
            for si, (off, w) in enumerate(subs):
                act(P, xa[:, m, off:off + w], xa[:, m, off:off + w], AF.Identity, [xab[m][si]], [xab[m][si]], scale=ALPHA)
        for (off, w) in subs:
            P.dma("sp", posi[:, 0:w], dr["pos"][:, t0 + off:t0 + off + w].partition_broadcast(128), writes=[posib], key="pin")
            i = nxt("rot", 2)
            cp(P, "dve", rot[i][:, :w], posi[:, 0:w], [posib], [rotb[i]])
            ts(P, "dve", rot[i][:, :w], rot[i][:, :w], invf[:, 0:1], None, ALU.mult, None, [rotb[i], invfb], [rotb[i]])
            sincos(C, rot[i], rotb[i], sin_t[:, off:off + w], sinb, cos_t[:, off:off + w], cosb,
                   [tmpf[0], tmpf[1]], [tmpfb[0], tmpfb[1]], sq[0][:].bitcast(I32), sqb[0], w)

        wglu = dr["w_glu"]
        for m in range(8):
            wsl, wsb = ws.load(wglu[:, m * 128:(m + 1) * 128], 8)
            for si, (off, w) in enumerate(subs):
                bk, bkb = C.bank4()
                for k in range(8):
                    mm(P, bk[:, :w], wsl[:, k, :], gb[:, 8 + k, off:off + w], k == 0, k == 7, [wsb, gbb[8 + k][si]], [bkb])
                i = nxt("tmpf", 2)
                act(P, tmpf[i][:, :w], bk[:, :w], AF.Sigmoid, [bkb, bglub], [tmpfb[i]], bias=bglu[:, m:m + 1])
                tt(P, "dve", hb[:, m, off:off + w], gb[:, 8 + m, off:off + w], tmpf[i][:, :w], ALU.mult,
                   [gbb[8 + m][si], tmpfb[i]], [hbb[m][si]])
        wo = dr["w_out_ab"]
        for m in range(16):
            wsl, wsb = ws.load(wo[:, m * 128:(m + 1) * 128], 16)
            for si, (off, w) in enumerate(subs):
                bk, bkb = C.bank4()
                for k in range(8):
                    mm(P, bk[:, :w], wsl[:, k, :], gb[:, k, off:off + w], k == 0, False, [wsb, gbb[k][si]], [bkb])
                for k in range(8):
                    mm(P, bk[:, :w], wsl[:, 8 + k, :], hb[:, k, off:off + w], False, k == 7, [wsb, hbb[k][si]], [bkb])
                resid(bk, bkb, m, si, off, w, mcol(0, 2, m), True, m == 0, m == 15)
        for si, (off, w) in enumerate(subs):
            ln_finish(si, off, w, 0, 1, 0, 4, 3, False)
        ffn(0, subs, 5, 2, 3, (1, 1, 0), False)

        wi = dr["w_in_c"]
        def proj_rot(wsl, wsb, dst_fn, dst_bufs_fn):
            for si, (off, w) in enumerate(subs):
                bk, bkb = C.bank4()
                for k in range(16):
                    mm(P, bk[:, :w], wsl[:, k, :], hb[:, k, off:off + w], k == 0, k == 15, [wsb, hbb[k][si]], [bkb])
                i = nxt("tmpf", 2)
                act(P, tmpf[i][:, :w], bk[:, :w], AF.Identity, [bkb], [tmpfb[i]])
                b2, b2b = C.bank4()
                mm(P, b2[:, :w], perm[:], tmpf[i][:, :w], True, True, [permb, tmpfb[i]], [b2b])
                r1 = nxt("rot", 2)
                tt(P, "dve", rot[r1][:, :w], tmpf[i][:, :w], cos_t[:, off:off + w], ALU.mult, [tmpfb[i], cosb], [rotb[r1]])
                r2 = nxt("rot", 2)
                tt(P, "dve", rot[r2][:, :w], b2[:, :w], sin_t[:, off:off + w], ALU.mult, [b2b, sinb], [rotb[r2]])
                tt(P, "dve", dst_fn(off, w), rot[r1][:, :w], rot[r2][:, :w], ALU.add, [rotb[r1], rotb[r2]],
                   dst_bufs_fn(si, off, w))

        for kv in range(4):
            c0 = 2048 + kv * 64
            ws.load(wi[:, c0:c0 + 64], 16, col0=0)
            wsl, wsb = ws.load(wi[:, c0:c0 + 64], 16, col0=64, fresh=False)
            proj_rot(wsl, wsb, lambda off, w, kv=kv: kT2[:, kv, 128 + off:128 + off + w],
                     lambda si, off, w: [kT2b[1 + (off + o) // 128] for o in range(0, w, 128)])
        wv0, wv0b = ws.load(wi[:, 2304:2432], 16)
        wv1, wv1b = ws.load(wi[:, 2432:2560], 16)
        for tb in range(nblk):
            si = tb // 4
            bk, bkb = C.bank4()
            for (wvl, wvb, cc) in ((wv0, wv0b, 0), (wv1, wv1b, 128)):
                for k in range(16):
                    mm(P, bk[:, cc:cc + 128], hb[:, k, tb * 128:(tb + 1) * 128], wvl[:, k, :], k == 0, k == 15,
                       [hbb[k][si], wvb], [bkb])
            act(P, vsb[:, 1 + tb, :], bk[:, 0:256], AF.Identity, [bkb], [vsbb[1 + tb]])
        if not halo:
            for m in range(16):
                wsl, wsb = ws.load(wi[:, m * 128:(m + 1) * 128], 16)
                proj_rot(wsl, wsb, lambda off, w, m=m: gb[:, m, off:off + w],
                         lambda si, off, w, m=m: [gbb[m][si]])
            steps = [(b, kv, hh) for b in range(nblk) for kv in range(4) for hh in range(2)]
            fm_ = first_main

            def s_phase(idx):
                b, kv, hh = steps[idx]
                si = b // 4
                kcol = 0 if (fm_ and b == 0) else 1
                ps_ = slice(hh * 64, hh * 64 + 64)
                base = 4 * (idx % 2)
                bS = [C.banks[base + 0], C.banks[base + 1]]
                bSb = [C.bbuf[base + 0], C.bbuf[base + 1]]
                qv = gb[ps_, 4 * kv:4 * kv + 4, b * 128:(b + 1) * 128]
                qbufs = [gbb[4 * kv + c][si] for c in range(4)]
                pis = []
                for which in range(2):
                    kb = b + which
                    mm(P, bS[which][:, :], kT2[ps_, kv, kb * 128:(kb + 1) * 128], qv, True, True,
                       [kT2b[kb]] + qbufs, [bSb[which]])
                    pi = nxt("pex", 4)
                    pis.append(pi)
                    bcol = kcol if which == 0 else 1
                    act(P, pex[pi][:].rearrange("p a b -> p (a b)"), bS[which][:, :], AF.Exp,
                        [bSb[which], kbiasb], [pexb[pi]], bias=kbias[:, bcol:bcol + 1], scale=0.125)
                    msk, mskb = (mprev, mprevb) if which == 0 else (mcur, mcurb)
                    tt(P, "dve", pex[pi][:], pex[pi][:], msk[:], ALU.mult, [pexb[pi], mskb], [pexb[pi]])
                return pis

            def pv_phase(idx, pis):
                b, kv, hh = steps[idx]
                si = b // 4
                ps_ = slice(hh * 64, hh * 64 + 64)
                base = 4 * (idx % 2)
                bO, bOb = C.banks[base + 2], C.bbuf[base + 2]
                bZ, bZb = C.banks[base + 3], C.bbuf[base + 3]
                for which in range(2):
                    kb = b + which
                    pv = pex[pis[which]][:].rearrange("p a b -> p (a b)")
                    mm(P, bO[ps_, :], vsb[:, kb, kv * 64:(kv + 1) * 64], pv, which == 0, which == 1,
                       [vsbb[kb], pexb[pis[which]]], [bOb])
                for which in range(2):
                    pv = pex[pis[which]][:].rearrange("p a b -> p (a b)")
                    mm(P, bZ[ps_, :], ones64[:], pv, which == 0, which == 1,
                       [ones64b, pexb[pis[which]]], [bZb])
                di = nxt("den", 2)
                dv = den[di][ps_]
                tt(P, "dve", dv, bZ[ps_, :].rearrange("p (a b) -> p a b", a=4),
                   es[ps_, 8 * kv + 4 * hh:8 * kv + 4 * hh + 4].unsqueeze(2).broadcast_to([64, 4, 128]), ALU.add,
                   [bZb, esb], [denb[di]])
                P.op("dve", lambda e: e.reciprocal(out=dv, in_=dv), reads=[denb[di]], writes=[denb[di]])
                tt(P, "dve", hb[ps_, 4 * kv:4 * kv + 4, b * 128:(b + 1) * 128],
                   bO[ps_, :].rearrange("p (a b) -> p a b", a=4), dv, ALU.mult,
                   [bOb, denb[di]], [hbb[4 * kv + c][si] for c in range(4)])

            prev = None
            for idx in range(len(steps)):
                pis = s_phase(idx)
                if prev is not None:
                    pv_phase(*prev)
                prev = (idx, pis)
            pv_phase(*prev)
            first_main = False
        if gi + 1 < len(groups):
            cp(P, "dve", kT2[:, :, 0:128], kT2[:, :, tg:tg + 128], [kT2b[nblk]], [kT2b[0]])
            cp(P, "dve", vsb[:, 0, :], vsb[:, nblk, :], [vsbb[nblk]], [vsbb[0]])
        if halo:
            continue
        wo = dr["w_out_c"]
        for m in range(16):
            wsl, wsb = ws.load(wo[:, m * 128:(m + 1) * 128], 16)
            for si, (off, w) in enumerate(subs):
                bk, bkb = C.bank4()
                for k in range(16):
                    mm(P, bk[:, :w], wsl[:, k, :], hb[:, k, off:off + w], k == 0, k == 15, [wsb, hbb[k][si]], [bkb])
                resid(bk, bkb, m, si, off, w, mcol(1, 2, m), True, m == 0, m == 15)
        for si, (off, w) in enumerate(subs):
            ln_finish(si, off, w, 4, 5, 1, 4, 3, False)
        ffn(1, subs, 5, 6, 7, (1, 1, 0), True)
        o = P.dma("sp", yT_v[:, :, t0 - HALO:t0 - HALO + tg], xa[:, :, 0:tg], reads=allx, key="yout")
        out_ops.append(o)
    return out_ops


def make_cfg(L, **kw):
    ntok = L // NCORE
    NT = ntok + HALO
    TG = min(1024, ntok)
    groups = [dict(t0=0, tg=HALO, halo=True)]
    for t in range(HALO, NT, TG):
        groups.append(dict(t0=t, tg=min(TG, NT - t), halo=False))
    d = dict(L=L, NTOK=ntok, NT=NT, TG=TG, groups=groups)
    d.update(kw)
    return d


def fm(v):
    return np.ascontiguousarray(np.asarray(v, np.float32).reshape(-1, 128).T)


def const_perm():
    p = np.zeros((128, 128), np.float32)
    for d in range(128):
        r = d % 64
        if r < 8:
            p[d + 8, d] = -1.0
        elif r < 16:
            p[d - 8, d] = 1.0
    return p


def const_invf():
    v = np.zeros((128, 1), np.float32)
    inv = np.power(np.float32(500000.0), -np.arange(8, dtype=np.float32) * np.float32(2.0 / 16)).astype(np.float32)
    for d in range(128):
        r = d % 64
        if r < 16:
            v[d, 0] = inv[r % 8]
    return v


B_INPUTS = [
    ("xT", lambda c: [2048, c["NT"]], F32), ("amT", lambda c: [16, 128, c["NT"]], BF16),
    ("cT", lambda c: [128, 16], F32), ("pos", lambda c: [1, c["NT"]], I32), ("kbias", lambda c: [128, 2], F32),
    ("invf", lambda c: [128, 1], F32), ("perm", lambda c: [128, 128], F32), ("sinkb", lambda c: [128, 32], F32),
    ("b_gluT", lambda c: [128, 8], F32), ("lnp", lambda c: [128, 8, 16], F32),
    ("w_glu", lambda c: [1024, 1024], F32), ("w_out_ab", lambda c: [2048, 2048], F32),
    ("w_in_c", lambda c: [2048, 2560], F32), ("w_out_c", lambda c: [2048, 2048], F32),
    ("w_ada", lambda c: [2, 2048, 12288], F32), ("b_ada", lambda c: [2, 1, 12288], F32),
    ("wg", lambda c: [2, 2048, DFF], F32), ("wu", lambda c: [2, 2048, DFF], F32), ("wd", lambda c: [2, DFF, 2048], F32),
]


def build_b(cfg):
    nc = bass.Bass("TRN2", target_bir_lowering=False)
    dr = {}
    for name, shp, dt in B_INPUTS:
        dr[name] = nc.dram_tensor(name, shp(cfg), dt, kind="ExternalInput").ap()
    dr["yT"] = nc.dram_tensor("yT", [2048, cfg["NTOK"]], F32, kind="ExternalOutput").ap()
    C = Ctx(nc)
    with contextlib.ExitStack() as st:
        C.alloc_banks(st)
        outs = phase_b(C, st, dr, cfg)
        C.P.emit(final_wait_ops=outs)
    return nc


def prep_b(inp, amT_full, cfg):
    L, NTOK, NT = cfg["L"], cfg["NTOK"], cfg["NT"]
    x = np.asarray(inp["x"], np.float32)[0]
    pos = np.asarray(inp["positions"], np.int32)[0]
    sinks = np.asarray(inp["attn_sinks"], np.float32)[0]
    sk = np.zeros(32, np.float32)
    for kv in range(4):
        for hh in range(2):
            for c in range(4):
                sk[kv * 8 + hh * 4 + c] = sinks[8 * kv + 2 * c + hh]
    lnp = np.stack([fm(inp["ln_mix_g"][0]), fm(inp["ln_mix_b"][0]), fm(inp["ln_ffn_g"][0]), fm(inp["ln_ffn_b"][0]),
                    fm(inp["ln_mix_g"][1]), fm(inp["ln_mix_b"][1]), fm(inp["ln_ffn_g"][1]), fm(inp["ln_ffn_b"][1])], axis=1)
    shared = dict(
        cT=fm(inp["c"][0]), invf=const_invf(), perm=const_perm(),
        sinkb=np.ascontiguousarray(np.broadcast_to(sk[None, :], (128, 32))),
        b_gluT=np.ascontiguousarray(np.asarray(inp["b_glu"], np.float32)[0].reshape(8, 128).T),
        lnp=np.ascontiguousarray(lnp),
        w_glu=np.asarray(inp["w_glu"], np.float32)[0], w_out_ab=np.asarray(inp["w_out_ab"], np.float32)[0],
        w_in_c=np.asarray(inp["w_in_c"], np.float32)[0], w_out_c=np.asarray(inp["w_out_c"], np.float32)[0],
        w_ada=np.asarray(inp["w_ada"], np.float32), b_ada=np.asarray(inp["b_ada"], np.float32)[:, None, :],
        wg=np.asarray(inp["w_ffn_gate"], np.float32), wu=np.asarray(inp["w_ffn_up"], np.float32),
        wd=np.asarray(inp["w_ffn_down"], np.float32),
    )
    maps = []
    for j in range(NCORE):
        lo = j * NTOK - HALO
        xT = np.zeros((2048, NT), np.float32)
        am = np.zeros((16, 128, NT), ml_dtypes.bfloat16)
        ps = np.zeros((1, NT), np.int32)
        s0 = max(lo, 0)
        xT[:, s0 - lo:] = x[s0:lo + NT].T
        if amT_full is not None:
            am[:, :, s0 - lo:] = amT_full[:, :, s0:lo + NT]
        ps[0, s0 - lo:] = pos[s0:lo + NT]
        kb = np.zeros((128, 2), np.float32)
        if j == 0:
            kb[:, 0] = -30000.0
        m = dict(shared)
        m.update(xT=xT, amT=am, pos=ps, kbias=kb)
        maps.append(m)
    return maps


def gather_b(res, cfg):
    outs = [np.asarray(r["yT"]).T for r in res.results]
    return np.concatenate(outs, axis=0)[None].astype(np.float32)


A_INPUTS = [
    ("xT", lambda c: [2048, c["L"]], F32), ("cT", lambda c: [128, 16], F32),
    ("w_inA", lambda c: [2048, 513], F32), ("w_ada", lambda c: [2, 2048, 12288], F32), ("b_ada", lambda c: [2, 1, 12288], F32),
    ("bf", lambda c: [128, 1], F32), ("ssm_s", lambda c: [128, 3, 8], F32),
    ("ssm_b", lambda c: [128, 2, 8, 16], F32), ("ssm_c", lambda c: [128, 2, 8, 16], F32),
    ("dcol", lambda c: [128, 1], F32), ("hsel", lambda c: [128, 5], F32), ("gmask", lambda c: [128, 8], F32),
    ("iota_t", lambda c: [128, 512], F32), ("swapI", lambda c: [128, 128], F32), ("ident", lambda c: [128, 128], F32),
]


def phase_a(C, st, dr, cfg):
    P = C.P
    L = cfg["L"]
    NBLK = L // 128
    NQ = L // 512
    PCH = 256
    sb = lambda name, shape, dt: C.sb(st, name, shape, dt)

    qT = sb("qT", [128, L], BF16)
    kT = sb("kT", [128, L], BF16)
    uT = sb("uT", [128, L], BF16)
    vtok = sb("vtok", [128, NBLK, 128], BF16)
    flog = sb("flog", [128, NBLK], F32)
    qTb = [Buf() for _ in range(L // PCH)]
    kTb = [Buf() for _ in range(L // PCH)]
    uTb = [Buf() for _ in range(L // PCH)]
    vtb = [Buf() for _ in range(L // PCH)]
    flogb = Buf()

    def pcb(bufs, t0, t1):
        return [bufs[i] for i in range(t0 // PCH, (t1 + PCH - 1) // PCH)]

    ones32 = sb("ones32", [128, 128], F32); ones32b = Buf()
    mset(P, "dve", ones32[:], 1.0, [ones32b])
    one11 = sb("one11", [1, 1], F32); one11b = Buf()
    mset(P, "dve", one11[:], 1.0, [one11b])
    mdiag = sb("mdiag", [128, 128], BF16); mdiagb = Buf()
    mset(P, "pool", mdiag[:], 1.0, [mdiagb])
    asel(P, mdiag[:], mdiag[:], [[1, 128]], ALU.is_ge, 0.0, 0, -1, [mdiagb], [mdiagb])

    def ld(name, shape, dt=F32):
        t = sb(name, shape, dt)
        b = Buf()
        P.dma("sp", t[:], dr[name], writes=[b], key="ca_" + name)
        return t, b

    cT, cTb = ld("cT", [128, 16])
    bfc, bfcb = ld("bf", [128, 1])
    ts(P, "dve", bfc[:], bfc[:], -1.0, None, ALU.mult, None, [bfcb], [bfcb])
    dcol, dcolb = ld("dcol", [128, 1])
    hsel, hselb = ld("hsel", [128, 5])
    gmask, gmaskb = ld("gmask", [128, 8])
    ident, identb = ld("ident", [128, 128])
    scb = sb("scb", [128, 16], BF16); scbb = Buf()
    act(P, scb[:], cT[:], AF.Silu, [cTb], [scbb])
    modT = sb("modTa", [128, 32], F32); modb = Buf()

    with contextlib.ExitStack() as st1:
        with contextlib.ExitStack() as st2:
            gen_mod(C, st2, dr["w_ada"][0], dr["b_ada"][0], scb, scbb, modT, modb, [0, 1], one11, one11b, "A")
            P.barrier()
        ts(P, "dve", modT[:, 16:32], modT[:, 16:32], 1.0, None, ALU.add, None, [modb], [modb])
        win = C.sb(st1, "win", [128, 16, 513], BF16); winb = Buf()
        if "zpad_ap" in cfg:
            zt = C.sb(st1, "zt", [128, HALO], BF16); ztb = Buf()
            mset(P, "dve", zt[:], 0.0, [ztb])
            for zi, zap in enumerate(cfg["zpad_ap"]):
                zb = Buf()
                cfg.setdefault("a_out_bufs", []).append(zb)
                P.dma("sp", zap, zt[:], reads=[ztb], writes=[zb], key=f"zpad{zi}")
        P.dma("pool", win[:], dr["w_inA"].rearrange("(k p) n -> p k n", p=128), writes=[winb], key="win")
        NXF = 2
        xf = [C.sb(st1, f"xf{i}", [128, 16, PCH], F32) for i in range(NXF)]
        xfb = [[Buf() for _ in range(4)] for _ in range(NXF)]
        hT = [C.sb(st1, f"hT{i}", [128, 16, PCH], BF16) for i in range(2)]
        hTb = [[Buf() for _ in range(16)] for _ in range(2)]
        xT_v = dr["xT"].rearrange("(k p) t -> p k t", p=128)
        for ch in range(L // PCH):
            i = ch % 2
            xi = ch % NXF
            t0 = ch * PCH
            for qd in range(4):
                P.dma("sp", xf[xi][:, 4 * qd:4 * qd + 4, :], xT_v[:, 4 * qd:4 * qd + 4, t0:t0 + PCH], writes=[xfb[xi][qd]],
                      key=f"xf{xi}_{qd}")
            for k in range(16):
                if k % 2 == 0:
                    act(P, hT[i][:, k, :], xf[xi][:, k, :], AF.Identity, [xfb[xi][k // 4], modb], [hTb[i][k]],
                        bias=modT[:, k:k + 1], scale=modT[:, 16 + k:17 + k])
                else:
                    ts(P, "dve", hT[i][:, k, :], xf[xi][:, k, :], modT[:, 16 + k:17 + k], modT[:, k:k + 1], ALU.mult, ALU.add,
                       [xfb[xi][k // 4], modb], [hTb[i][k]])
            for (dst, dbufs, c0, eng) in ((qT, qTb, 0, "act"), (kT, kTb, 128, "dve"), (uT, uTb, 256, "act")):
                bk, bkb = C.bank4()
                for k in range(16):
                    mm(P, bk[:, :PCH], win[:, k, c0:c0 + 128], hT[i][:, k, :], k == 0, k == 15, [winb, hTb[i][k]], [bkb])
                if eng == "act":
                    act(P, dst[:, t0:t0 + PCH], bk[:, :PCH], AF.Identity, [bkb], [dbufs[ch]])
                else:
                    cp(P, "dve", dst[:, t0:t0 + PCH], bk[:, :PCH], [bkb], [dbufs[ch]])
            for tb in range(PCH // 128):
                bk, bkb = C.bank4()
                for k in range(16):
                    mm(P, bk[:, 0:129], hT[i][:, k, tb * 128:(tb + 1) * 128], win[:, k, 384:513], k == 0, k == 15,
                       [winb, hTb[i][k]], [bkb])
                blk = t0 // 128 + tb
                cp(P, "dve", vtok[:, blk, :], bk[:, 0:128], [bkb], [vtb[ch]])
                cp(P, "dve", flog[:, blk:blk + 1], bk[:, 128:129], [bkb], [flogb])
        P.barrier()

    sb_main = sb
    pre = {}
    for (nm, shp, dt_) in (("Fsb", [128, NBLK], F32), ("Fend", [128, NQ], F32), ("sm", [128, 16, 8], F32),
                           ("Bm0", [128, 8, 128], BF16), ("Bm1", [128, 8, 128], BF16), ("Cz0", [128, 8, 128], BF16),
                           ("Cz1", [128, 8, 128], BF16), ("Dm", [128, 128], BF16), ("Rm", [128, 8, 128], F32),
                           ("Tc", [128, 8, 512], F32), ("Ts", [128, 8, 512], F32)):
        pre[nm] = sb_main(nm, shp, dt_)
    stS = contextlib.ExitStack()
    sb = lambda name, shape, dt: pre[name] if name in pre else C.sb(stS, name, shape, dt)
    uneg = sb("uneg", [128, 128], F32); unegb = Buf()
    mset(P, "pool", uneg[:], -1.0, [unegb])
    asel(P, uneg[:], uneg[:], [[1, 128]], ALU.is_ge, 0.0, 0, -1, [unegb], [unegb])
    suneg = sb("suneg", [128, 128], F32); sunegb = Buf()
    mset(P, "pool", suneg[:], -1.0, [sunegb])
    asel(P, suneg[:], suneg[:], [[1, 128]], ALU.is_gt, 0.0, 0, -1, [sunegb], [sunegb])
    e127 = sb("e127", [128, 128], F32); e127b = Buf()
    mset(P, "pool", e127[:], 1.0, [e127b])
    asel(P, e127[:], e127[:], [[0, 128]], ALU.is_ge, 0.0, -127, 1, [e127b], [e127b])
    ssm_s, ssm_sb = ld("ssm_s", [128, 3, 8])
    ssm_b, ssm_bb = ld("ssm_b", [128, 2, 8, 16])
    ssm_c, ssm_cb = ld("ssm_c", [128, 2, 8, 16])
    iota_t, iotab = ld("iota_t", [128, 512])
    swapI, swapIb = ld("swapI", [128, 128])
    lsm = sb("lsm", [128, NBLK], F32); lsmb = Buf()
    act(P, lsm[:], flog[:], AF.Exp, [flogb, bfcb], [lsmb], bias=bfc[:, 0:1], scale=-1.0)
    ts(P, "dve", lsm[:], lsm[:], 1.0, None, ALU.add, None, [lsmb], [lsmb])
    act(P, lsm[:], lsm[:], AF.Ln, [lsmb], [lsmb])
    Fb, Fbb = C.banks[4], C.bbuf[4]
    Tb_, Tbb = C.banks[5], C.bbuf[5]
    mm(P, Tb_[0:NBLK, 0:128], lsm[:, 0:NBLK], ones32[:], True, True, [lsmb, ones32b], [Tbb])
    totb = sb("totb", [128, 128], F32); totbb = Buf()
    cp(P, "dve", totb[0:NBLK, :], Tb_[0:NBLK, 0:128], [Tbb], [totbb])
    mm(P, Fb[:, 0:NBLK], uneg[:], lsm[:, 0:NBLK], True, False, [unegb, lsmb], [Fbb])
    mm(P, Fb[:, 0:NBLK], totb[0:NBLK, :], suneg[0:NBLK, 0:NBLK], False, True, [totbb, sunegb], [Fbb])
    Fsb = sb("Fsb", [128, NBLK], F32); Fsbb = Buf()
    cp(P, "dve", Fsb[:], Fb[:, 0:NBLK], [Fbb], [Fsbb])
    mm(P, Tb_[:, 0:NQ], e127[:], Fsb[:].rearrange("p (g f) -> p g f", f=4)[:, :, 3], True, True, [e127b, Fsbb], [Tbb])
    Fend = sb("Fend", [128, NQ], F32); Fendb = Buf()
    cp(P, "dve", Fend[:], Tb_[:, 0:NQ], [Tbb], [Fendb])

    lamre, lamim, ldt = ssm_s[:, 0, :], ssm_s[:, 1, :], ssm_s[:, 2, :]
    sm = sb("sm", [128, 16, 8], F32)
    smb = [Buf() for _ in range(16)]
    DT, ARE, TH, RHO, SN, CS, LBR, LBI, RDEN, NR, QRE, QIM, QA, QB, QA2, TMP = range(16)
    V = lambda i: sm[:, i, :]
    act(P, V(DT), ldt, AF.Exp, [ssm_sb], [smb[DT]])
    tt(P, "dve", V(ARE), lamre, V(DT), ALU.mult, [ssm_sb, smb[DT]], [smb[ARE]])
    tt(P, "dve", V(TH), lamim, V(DT), ALU.mult, [ssm_sb, smb[DT]], [smb[TH]])
    act(P, V(RHO), V(ARE), AF.Exp, [smb[ARE]], [smb[RHO]])
    sc_tmp = [sb(f"sct{i}", [128, 8], F32) for i in range(2)]
    sc_tmpb = [Buf() for _ in range(2)]
    sc_ki = sb("scki", [128, 8], I32); sc_kib = Buf()
    sincos(C, V(TH), smb[TH], V(SN), smb[SN], V(CS), smb[CS], sc_tmp, sc_tmpb, sc_ki, sc_kib, 8)
    tt(P, "dve", V(LBR), V(RHO), V(CS), ALU.mult, [smb[RHO], smb[CS]], [smb[LBR]])
    tt(P, "dve", V(LBI), V(RHO), V(SN), ALU.mult, [smb[RHO], smb[SN]], [smb[LBI]])
    tt(P, "dve", V(RDEN), lamre, lamre, ALU.mult, [ssm_sb], [smb[RDEN]])
    tt(P, "dve", V(TMP), lamim, lamim, ALU.mult, [ssm_sb], [smb[TMP]])
    tt(P, "dve", V(RDEN), V(RDEN), V(TMP), ALU.add, [smb[RDEN], smb[TMP]], [smb[RDEN]])
    P.op("dve", lambda e: e.reciprocal(out=V(RDEN), in_=V(RDEN)), reads=[smb[RDEN]], writes=[smb[RDEN]])
    ts(P, "dve", V(NR), V(LBR), -1.0, None, ALU.add, None, [smb[LBR]], [smb[NR]])
    tt(P, "dve", V(QRE), V(NR), lamre, ALU.mult, [smb[NR], ssm_sb], [smb[QRE]])
    tt(P, "dve", V(TMP), V(LBI), lamim, ALU.mult, [smb[LBI], ssm_sb], [smb[TMP]])
    tt(P, "dve", V(QRE), V(QRE), V(TMP), ALU.add, [smb[QRE], smb[TMP]], [smb[QRE]])
    tt(P, "dve", V(QRE), V(QRE), V(RDEN), ALU.mult, [smb[QRE], smb[RDEN]], [smb[QRE]])
    tt(P, "dve", V(QIM), V(LBI), lamre, ALU.mult, [smb[LBI], ssm_sb], [smb[QIM]])
    tt(P, "dve", V(TMP), V(NR), lamim, ALU.mult, [smb[NR], ssm_sb], [smb[TMP]])
    tt(P, "dve", V(QIM), V(QIM), V(TMP), ALU.subtract, [smb[QIM], smb[TMP]], [smb[QIM]])
    tt(P, "dve", V(QIM), V(QIM), V(RDEN), ALU.mult, [smb[QIM], smb[RDEN]], [smb[QIM]])
    top, bot, ntop, nbot, tmb = (hsel[:, i:i + 1] for i in range(5))
    ts(P, "dve", V(TMP), V(QIM), bot, None, ALU.mult, None, [smb[QIM], hselb], [smb[TMP]])
    stt(P, V(QA), V(QRE), top, V(TMP), ALU.mult, ALU.add, [smb[QRE], smb[TMP], hselb], [smb[QA]])
    ts(P, "dve", V(TMP), V(QRE), bot, None, ALU.mult, None, [smb[QRE], hselb], [smb[TMP]])
    stt(P, V(QB), V(QIM), ntop, V(TMP), ALU.mult, ALU.add, [smb[QIM], smb[TMP], hselb], [smb[QB]])
    stt(P, V(QA2), V(QIM), top, V(TMP), ALU.mult, ALU.subtract, [smb[QIM], smb[TMP], hselb], [smb[QA2]])
    bre2, bim2 = ssm_b[:, 0], ssm_b[:, 1]
    cre2, cim2 = ssm_c[:, 0], ssm_c[:, 1]
    bc = lambda i: V(i).unsqueeze(2).broadcast_to([128, 8, 16])
    M1 = sb("M1", [128, 8, 16], F32); M1b = Buf()
    M2 = sb("M2", [128, 8, 16], F32); M2b = Buf()
    Mt = sb("Mt", [128, 8, 16], F32); Mtb = Buf()
    tt(P, "dve", M1[:], bre2, bc(QA), ALU.mult, [ssm_bb, smb[QA]], [M1b])
    tt(P, "dve", Mt[:], bim2, bc(QB), ALU.mult, [ssm_bb, smb[QB]], [Mtb])
    tt(P, "dve", M1[:], M1[:], Mt[:], ALU.add, [M1b, Mtb], [M1b])
    tt(P, "dve", M2[:], bre2, bc(QA2), ALU.mult, [ssm_bb, smb[QA2]], [M2b])
    tt(P, "dve", Mt[:], bim2, bc(QA), ALU.mult, [ssm_bb, smb[QA]], [Mtb])
    tt(P, "dve", M2[:], M2[:], Mt[:], ALU.add, [M2b, Mtb], [M2b])
    Bm = [sb(f"Bm{i}", [128, 8, 128], BF16) for i in range(2)]
    Bmb = [Buf() for _ in range(2)]
    for i, (M, Mb) in enumerate(((M1, M1b), (M2, M2b))):
        bk, bkb = C.bank4()
        P.op("pe", lambda e, M=M, bk=bk: e.transpose(bk[:, 0:128], M[:].rearrange("p g h -> p (g h)"), ident[:]),
             reads=[Mb, identb], writes=[bkb])
        for g in range(8):
            ts(P, "dve", Bm[i][:, g, :], bk[:, 0:128], gmask[:, g:g + 1], None, ALU.mult, None, [bkb, gmaskb], [Bmb[i]])
    Cf = [sb(f"Cf{i}", [128, 8, 16], F32) for i in range(2)]
    Cfb = [Buf() for _ in range(2)]
    ts(P, "dve", Mt[:], cim2, bot, None, ALU.mult, None, [ssm_cb, hselb], [Mtb])
    stt(P, Cf[0][:], cre2, top, Mt[:], ALU.mult, ALU.subtract, [ssm_cb, Mtb, hselb], [Cfb[0]])
    ts(P, "dve", Mt[:], cre2, bot, None, ALU.mult, None, [ssm_cb, hselb], [Mtb])
    stt(P, Cf[1][:], cim2, ntop, Mt[:], ALU.mult, ALU.subtract, [ssm_cb, Mtb, hselb], [Cfb[1]])
    Cz = [sb(f"Cz{i}", [128, 8, 128], BF16) for i in range(2)]
    Czb = [Buf() for _ in range(2)]
    for i in range(2):
        mset(P, "pool", Cz[i][:], 0.0, [Czb[i]])
        for g in range(8):
            cp(P, "dve", Cz[i][:, g, g * 16:(g + 1) * 16], Cf[i][:, g, :], [Cfb[i]], [Czb[i]])
    Dm = sb("Dm", [128, 128], BF16); Dmb = Buf()
    ts(P, "dve", Dm[:], ident[:], dcol[:, 0:1], None, ALU.mult, None, [identb, dcolb], [Dmb])
    th5 = sb("th5", [128, 8], F32); th5b = Buf()
    ts(P, "dve", th5[:], V(TH), 512.0, None, ALU.mult, None, [smb[TH]], [th5b])
    sR = sb("sR", [128, 8], F32); sRb = Buf()
    cR = sb("cR", [128, 8], F32); cRb = Buf()
    sincos(C, th5, th5b, sR, sRb, cR, cRb, sc_tmp, sc_tmpb, sc_ki, sc_kib, 8)
    ts(P, "dve", sR[:], sR[:], tmb, None, ALU.mult, None, [sRb, hselb], [sRb])
    Rm = sb("Rm", [128, 8, 128], F32); Rmb = Buf()
    Rt = sb("Rt", [128, 128], F32); Rtb = Buf()
    for g in range(8):
        ts(P, "dve", Rt[:], swapI[:], sR[:, g:g + 1], None, ALU.mult, None, [swapIb, sRb], [Rtb])
        stt(P, Rm[:, g, :], ident[:], cR[:, g:g + 1], Rt[:], ALU.mult, ALU.add, [identb, cRb, Rtb], [Rmb])
    Tc = sb("Tc", [128, 8, 512], F32); Tcb = [Buf() for _ in range(8)]
    Ts = sb("Ts", [128, 8, 512], F32); Tsb = [Buf() for _ in range(8)]
    tabt = [sb(f"tabt{i}", [128, 512], F32) for i in range(3)]
    tabtb = [Buf() for _ in range(3)]
    tabk = sb("tabk", [128, 512], I32); tabkb = Buf()
    for g in range(8):
        ts(P, "dve", tabt[0][:], iota_t[:], sm[:, TH, g:g + 1], None, ALU.mult, None, [iotab, smb[TH]], [tabtb[0]])
        sincos(C, tabt[0], tabtb[0], Ts[:, g, :], Tsb[g], Tc[:, g, :], Tcb[g], [tabt[1], tabt[2]], [tabtb[1], tabtb[2]],
               tabk, tabkb, 512)

    P.barrier()
    stS.close()
    sb = sb_main
    NR_ = 2
    w1 = [sb(f"w1_{i}", [128, 512], F32) for i in range(NR_)]; w1b = [Buf() for _ in range(NR_)]
    wsc = [sb(f"wsc_{i}", [128, 512], F32) for i in range(NR_)]; wscb = [Buf() for _ in range(NR_)]
    p1 = [sb(f"p1_{i}", [128, 512], BF16) for i in range(NR_)]; p1b = [Buf() for _ in range(NR_)]
    p2 = [sb(f"p2_{i}", [128, 512], BF16) for i in range(NR_)]; p2b = [Buf() for _ in range(NR_)]
    wlast = sb("wlast", [128, 8], F32); wlastb = [Buf() for _ in range(8)]
    winit = sb("winit", [128, 8], F32); winitb = [Buf() for _ in range(8)]
    gy = [sb(f"gy{i}", [128, 512], F32) for i in range(2)]; gyb = [Buf() for _ in range(2)]
    obs = [sb(f"obs{i}", [128, 512], BF16) for i in range(2)]; obsb = [Buf() for _ in range(2)]
    oba = [sb(f"oba{i}", [128, 512], BF16) for i in range(2)]; obab = [Buf() for _ in range(2)]
    pex = [sb(f"pexa{i}", [128, 512], BF16) for i in range(3)]; pexb = [Buf() for _ in range(3)]
    accD = [sb(f"accD{i}", [128, 512], F32) for i in range(2)]; accDb = [Buf() for _ in range(2)]
    accP = [sb(f"accP{i}", [128, 512], F32) for i in range(2)]; accPb = [Buf() for _ in range(2)]
    nb = [sb(f"nb{i}", [128, NBLK], F32) for i in range(2)]; nbb = [Buf() for _ in range(2)]
    cnt = {"r": 0, "pex": 0}
    amA = dr["amA"]
    out_ops = []
    out_bufs = cfg.setdefault("a_out_bufs", [])
    Yb, Ybb = C.banks[4], C.bbuf[4]
    Ib, Ibb = C.banks[5], C.bbuf[5]
    scale = 1.0 / math.sqrt(128.0)

    SB3 = [(C.banks[i], C.bbuf[i]) for i in range(3)]
    B1, B1b = C.banks[3], C.bbuf[3]
    B2, B2b = C.banks[7], C.bbuf[7]
    Ob, Obb = C.banks[6], C.bbuf[6]
    LA = cfg.get('LA', 2)

    def ssm_gen(c):
        t0 = c * 512
        ub = pcb(uTb, t0, t0 + 512)

        def stage1(g, r):
            mm(P, B1[:, :], Bm[0][:, g, :], uT[:, t0:t0 + 512], True, True, [Bmb[0]] + ub, [B1b])
            mm(P, B2[:, :], Bm[1][:, g, :], uT[:, t0:t0 + 512], True, True, [Bmb[1]] + ub, [B2b])
            tt(P, "dve", w1[r][:], B1[:, :], Tc[:, g, :], ALU.mult, [B1b, Tcb[g]], [w1b[r]])
            tt(P, "dve", wsc[r][:], B2[:, :], Ts[:, g, :], ALU.mult, [B2b, Tsb[g]], [wscb[r]])
            tt(P, "pool", w1[r][:], w1[r][:], wsc[r][:], ALU.add, [w1b[r], wscb[r]], [w1b[r]])
            if c == 0:
                init = 0.0
                ird = []
            else:
                mm(P, Ib[:, g:g + 1], Rm[:, g, :], wlast[:, g:g + 1], True, True, [Rmb, wlastb[g]], [Ibb])
                cp(P, "dve", winit[:, g:g + 1], Ib[:, g:g + 1], [Ibb], [winitb[g]])
                init = winit[:, g:g + 1]
                ird = [winitb[g]]
            P.op("dve", lambda e: e.tensor_tensor_scan(
                out=wsc[r][:], data0=sm[:, RHO, g:g + 1].broadcast_to([128, 512]), data1=w1[r][:], initial=init,
                op0=ALU.mult, op1=ALU.add), reads=[smb[RHO], w1b[r]] + ird, writes=[wscb[r]])
            cp(P, "dve", wlast[:, g:g + 1], wsc[r][:, 511:512], [wscb[r]], [wlastb[g]])

        def stage2(g, r):
            tt(P, "pool", p1[r][:], wsc[r][:], Tc[:, g, :], ALU.mult, [wscb[r], Tcb[g]], [p1b[r]])
            tt(P, "pool", p2[r][:], wsc[r][:], Ts[:, g, :], ALU.mult, [wscb[r], Tsb[g]], [p2b[r]])
            mm(P, Yb[:, :], Cz[0][:, g, :], p1[r][:], g == 0, False, [Czb[0], p1b[r]], [Ybb])
            mm(P, Yb[:, :], Cz[1][:, g, :], p2[r][:], False, False, [Czb[1], p2b[r]], [Ybb])

        prev = None
        for g in range(8):
            r = cnt["r"] % NR_
            cnt["r"] += 1
            stage1(g, r)
            yield
            if not cfg.get('PIPE_SSM', True):
                stage2(g, r)
                yield
                continue
            if prev is not None:
                stage2(*prev)
                yield
            prev = (g, r)
        if prev is not None:
            stage2(*prev)
        mm(P, Yb[:, :], Dm[:], uT[:, t0:t0 + 512], False, True, [Dmb] + ub, [Ybb])
        yield
        act(P, gy[0][:], Yb[:, :], AF.Identity, [Ybb], [gyb[0]])
        act(P, gy[1][:], Yb[:, :], AF.Square, [Ybb], [gyb[1]])
        ts(P, "dve", gy[1][:], gy[1][:], 0.044715, 1.0, ALU.mult, ALU.add, [gyb[1]], [gyb[1]])
        tt(P, "dve", gy[1][:], gy[1][:], gy[0][:], ALU.mult, [gyb[1], gyb[0]], [gyb[1]])
        act(P, gy[1][:], gy[1][:], AF.Sigmoid, [gyb[1]], [gyb[1]], scale=2.0 * GELU_C)
        oi = c % 2
        tt(P, "dve", obs[oi][:], gy[0][:], gy[1][:], ALU.mult, [gyb[0], gyb[1]], [obsb[oi]])
        ob_ = Buf()
        out_bufs.append(ob_)
        out_ops.append(P.dma("sp", amA[1][:, t0:t0 + 512], obs[oi][:], reads=[obsb[oi]], writes=[ob_], key=f"oS{oi}"))
        yield

    def attn_gen(c):
        t0 = c * 512
        nkb = 4 * c + 4
        ni = c % 2
        ts(P, "dve", nb[ni][:, 0:nkb], Fsb[:, 0:nkb], -1.0, Fend[:, c:c + 1], ALU.mult, ALU.add, [Fsbb, Fendb], [nbb[ni]])
        mset(P, "pool", accD[ni][:], 0.0, [accDb[ni]])
        mset(P, "pool", accP[ni][:], 0.0, [accPb[ni]])
        qb_ = pcb(qTb, t0, t0 + 512)
        yield

        def emit_s(kb):
            j = kb - 4 * c
            c0 = 128 * j if j > 0 else 0
            bS, bSb = SB3[cnt["sb"] % 3]
            cnt["sb"] += 1
            kbufs = pcb(kTb, kb * 128, kb * 128 + 128)
            mm(P, bS[:, c0:512], kT[:, kb * 128:(kb + 1) * 128], qT[:, t0 + c0:t0 + 512], True, True, kbufs + qb_, [bSb])
            pi = cnt["pex"] % 3
            cnt["pex"] += 1
            act(P, pex[pi][:, c0:512], bS[:, c0:512], AF.Exp, [bSb, nbb[ni]], [pexb[pi]], bias=nb[ni][:, kb:kb + 1], scale=scale)
            if j >= 0:
                tt(P, "pool", pex[pi][:, c0:c0 + 128], pex[pi][:, c0:c0 + 128], mdiag[:], ALU.mult, [pexb[pi], mdiagb], [pexb[pi]])
            return (kb, c0, pi)

        def emit_pv(kb, c0, pi):
            mm(P, Ob[:, c0:512], vtok[:, kb, :], pex[pi][:, c0:512], kb == 0, kb == nkb - 1,
               pcb(vtb, kb * 128, kb * 128 + 128) + [pexb[pi]], [Obb])
            if kb % 2 == 0:
                tt(P, "dve", accD[ni][:, c0:512], accD[ni][:, c0:512], pex[pi][:, c0:512], ALU.add, [accDb[ni], pexb[pi]], [accDb[ni]])
            else:
                tt(P, "pool", accP[ni][:, c0:512], accP[ni][:, c0:512], pex[pi][:, c0:512], ALU.add, [accPb[ni], pexb[pi]], [accPb[ni]])

        pend = []
        for kb in range(nkb):
            pend.append(emit_s(kb))
            if len(pend) > LA:
                emit_pv(*pend.pop(0))
            yield
        while pend:
            emit_pv(*pend.pop(0))
        yield
        tt(P, "dve", accD[ni][:], accD[ni][:], accP[ni][:], ALU.add, [accDb[ni], accPb[ni]], [accDb[ni]])
        bZ, bZb = SB3[cnt["sb"] % 3]
        cnt["sb"] += 1
        mm(P, bZ[:, :], ones32[:], accD[ni][:], True, True, [ones32b, accDb[ni]], [bZb])
        P.op("dve", lambda e: e.reciprocal(out=accP[ni][:], in_=bZ[:, :]), reads=[bZb], writes=[accPb[ni]])
        tt(P, "dve", oba[ni][:], Ob[:, :], accP[ni][:], ALU.mult, [Obb, accPb[ni]], [obab[ni]])
        ob_ = Buf()
        out_bufs.append(ob_)
        out_ops.append(P.dma("sp", amA[0][:, t0:t0 + 512], oba[ni][:], reads=[obab[ni]], writes=[ob_], key=f"oA{ni}"))
        yield

    cnt["sb"] = 0
    for _ in ssm_gen(0):
        pass
    for c in range(NQ):
        ga = attn_gen(c)
        gs = ssm_gen(c + 1) if c + 1 < NQ else None
        n_a = 4 * c + 4 + 3
        n_s = 18
        every = max(1, n_a // n_s)
        step = 0
        if not cfg.get('INTERLEAVE', False):
            for _ in ga:
                pass
        for _ in ga:
            step += 1
            if gs is not None and step % every == 0:
                if next(gs, "done") == "done":
                    gs = None
        if gs is not None:
            for _ in gs:
                pass
    return out_ops


def build_a(cfg):
    nc = bass.Bass("TRN2", target_bir_lowering=False)
    dr = {}
    for name, shp, dt in A_INPUTS:
        dr[name] = nc.dram_tensor(name, shp(cfg), dt, kind="ExternalInput").ap()
    dr["amA"] = nc.dram_tensor("amA", [2, 128, cfg["L"]], BF16, kind="ExternalOutput").ap()
    C = Ctx(nc)
    with contextlib.ExitStack() as st:
        C.alloc_banks(st)
        outs = phase_a(C, st, dr, cfg)
        C.P.emit(final_wait_ops=outs)
    return nc


def prep_a(inp, cfg):
    L = cfg["L"]
    x = np.asarray(inp["x"], np.float32)[0]
    xT = np.ascontiguousarray(x.T)
    w_in = np.asarray(inp["w_in_ab"], np.float32)[0]
    w_ada = np.asarray(inp["w_ada"], np.float32)
    b_ada = np.asarray(inp["b_ada"], np.float32)[:, None, :]
    lre = np.asarray(inp["ssm_lambda_re"], np.float32)[0]
    lim = np.asarray(inp["ssm_lambda_im"], np.float32)[0]
    ldt = np.asarray(inp["ssm_log_dt"], np.float32)[0]
    bre = np.asarray(inp["ssm_b_re"], np.float32)[0]
    bim = np.asarray(inp["ssm_b_im"], np.float32)[0]
    cre = np.asarray(inp["ssm_c_re"], np.float32)[0]
    cim = np.asarray(inp["ssm_c_im"], np.float32)[0]
    dsk = np.asarray(inp["ssm_d"], np.float32)[0]
    bfg = np.asarray(inp["b_forget"], np.float32)[0]
    hsel = np.zeros((128, 5), np.float32)
    hsel[:64, 0] = 1; hsel[64:, 1] = 1; hsel[:64, 2] = -1; hsel[64:, 3] = -1; hsel[:64, 4] = 1; hsel[64:, 4] = -1
    gmask = np.zeros((128, 8), np.float32)
    for p in range(128):
        gmask[p, p // 16] = 1
    iota_t = np.ascontiguousarray(np.broadcast_to(np.arange(512, dtype=np.float32)[None], (128, 512)))
    swapI = np.zeros((128, 128), np.float32)
    for p in range(128):
        swapI[p, (p + 64) % 128] = 1
    ident = np.eye(128, dtype=np.float32)
    shared = dict(xT=xT, cT=fm(inp["c"][0]), w_ada=w_ada, b_ada=b_ada, hsel=hsel, gmask=gmask, iota_t=iota_t,
                  swapI=swapI, ident=ident)
    maps = []
    dup = lambda a: np.concatenate([a, a], axis=0)
    for i in range(NCORE):
        gs = slice(8 * i, 8 * i + 8)
        W = 1024
        cols = np.concatenate([w_in[:, i * 128:(i + 1) * 128], w_in[:, W + i * 128:W + (i + 1) * 128],
                               w_in[:, 3 * W + 8 + i * 128:3 * W + 8 + (i + 1) * 128],
                               w_in[:, 2 * W + i * 128:2 * W + (i + 1) * 128], w_in[:, 3 * W + i:3 * W + i + 1]], axis=1)
        ssm_s = np.stack([dup(lre[gs].T), dup(lim[gs].T), np.broadcast_to(ldt[gs][None, :], (128, 8))], axis=1)
        ssm_b = np.stack([dup(bre[gs].transpose(1, 0, 2)), dup(bim[gs].transpose(1, 0, 2))], axis=1)
        ssm_c = np.stack([dup(cre[gs].transpose(2, 0, 1)), dup(cim[gs].transpose(2, 0, 1))], axis=1)
        m = dict(shared)
        m.update(w_inA=np.ascontiguousarray(cols), bf=np.full((128, 1), bfg[i], np.float32),
                 ssm_s=np.ascontiguousarray(ssm_s.astype(np.float32)), ssm_b=np.ascontiguousarray(ssm_b),
                 ssm_c=np.ascontiguousarray(ssm_c), dcol=np.ascontiguousarray(dsk[gs].reshape(128, 1)))
        maps.append(m)
    return maps


def gather_a(res, cfg):
    L = cfg["L"]
    am = np.zeros((16, 128, L), ml_dtypes.bfloat16)
    for i, r in enumerate(res.results):
        a = np.asarray(r["amA"])
        am[i] = a[0]
        am[8 + i] = a[1]
    return am


_NC_CACHE = {}


def kernel_unfused(**inputs):
    L = np.asarray(inputs["x"]).shape[1]
    cfg = make_cfg(L)
    inp = {k: np.asarray(v) for k, v in inputs.items()}
    nca = build_a(cfg)
    ncb = build_b(make_cfg(L))
    resA = run_bass_kernel_spmd(nca, prep_a(inp, cfg), core_ids=list(range(NCORE)))
    amT = gather_a(resA, cfg)
    resB = run_bass_kernel_spmd(ncb, prep_b(inp, amT, cfg), core_ids=list(range(NCORE)))
    return gather_b(resB, cfg)


def build_fused(cfg):
    nc = bass.Bass("TRN2", target_bir_lowering=False)
    L, NTOK, NT = cfg["L"], cfg["NTOK"], cfg["NT"]
    dr = {}
    seen = set()
    for name, shp, dt in A_INPUTS:
        dr[name] = nc.dram_tensor(name, shp(cfg), dt, kind="ExternalInput").ap()
        seen.add(name)
    for name, shp, dt in B_INPUTS:
        if name == "amT" or name in ("cT", "w_ada", "b_ada"):
            continue
        nm = "xTb" if name == "xT" else name
        dr[nm] = nc.dram_tensor(nm, shp(cfg), dt, kind="ExternalInput").ap()
    dr["yT"] = nc.dram_tensor("yT", [2048, NTOK], F32, kind="ExternalOutput").ap()
    WC = HALO + L
    cins = [nc.dram_tensor(f"cc_in{h}", [128, WC], BF16) for h in range(2)]
    couts = [nc.dram_tensor(f"cc_out{h}", [NCORE * 128, WC], BF16) for h in range(2)]
    dr["amA"] = [cins[h].ap()[:, HALO:WC] for h in range(2)]
    C = Ctx(nc)
    P = C.P
    cfg["a_out_bufs"] = []
    with contextlib.ExitStack() as st:
        C.alloc_banks(st)
        with contextlib.ExitStack() as stA:
            cfg["zpad_ap"] = [cins[h].ap()[:, 0:HALO] for h in range(2)]
            phase_a(C, stA, dr, cfg)
            P.barrier()
        bcout = [Buf(), Buf()]
        for h in range(2):
            def cc(e, h=h):
                return e.collective_compute("AllGather", ALU.bypass, replica_groups=[list(range(NCORE))],
                                            ins=[cins[h].ap().opt()], outs=[couts[h].ap().opt()])
            P.op("pool", cc, reads=cfg["a_out_bufs"], writes=[bcout[h]], is_dma=True, semkey=f"cc{h}", inc=1)
        cout_v = [couts[h].ap().rearrange("(r p) t -> p r t", r=NCORE, p=128) for h in range(2)]

        def am_loader(P_, gb, t0, tg, wr):
            for h in range(2):
                def f(e, h=h):
                    pid = e.partition_id()
                    return e.dma_start(out=gb[:, 8 * h:8 * h + 8, 0:tg], in_=cout_v[h][:, :, bass.ds(pid * NTOK + t0, tg)])
                P_.op("sp", f, reads=[bcout[h]], writes=wr(h), is_dma=True, semkey="ain")
        cfgb = dict(cfg)
        cfgb["am_loader"] = am_loader
        cfgb["xb_name"] = "xTb"
        outs = phase_b(C, st, dr, cfgb)
        P.emit(final_wait_ops=outs)
    return nc


def kernel(**inputs):
    L = np.asarray(inputs["x"]).shape[1]
    cfg = make_cfg(L)
    inp = {k: np.asarray(v) for k, v in inputs.items()}
    nc = build_fused(cfg)
    ma = prep_a(inp, cfg)
    mb = prep_b(inp, None, cfg)
    maps = []
    for a, b in zip(ma, mb):
        m = dict(a)
        for k, v in b.items():
            if k in ("amT", "cT", "w_ada", "b_ada"):
                continue
            m["xTb" if k == "xT" else k] = v
        maps.append(m)
    res = run_bass_kernel_spmd(nc, maps, core_ids=list(range(NCORE)))
    return gather_b(res, cfg)
```
